# Optimizing a Trainium2 kernel written in Bass

```python
import math
import jax, jax.numpy as jnp
from jax import lax
import numpy as np

D_MODEL = 1024
BATCH = 1
SEQ = 16384
DEPTH = 4

GRID_W = 64
CTX_LEN = 256
N_MIXERS = 4
N_GM = (DEPTH + 3) // 4
N_AT = (DEPTH + 2) // 4
N_HY = (DEPTH + 1) // 4
N_SSD = DEPTH // 4
NORM_EPS = 1e-6
FFN_HIDDEN = -(-8 * D_MODEL // (3 * 256)) * 256

GM_CHUNK = 128
GM_WIDTH = 2 * D_MODEL
GM_GROUPS = 8

AT_HEAD_DIM = 64
AT_Q_HEADS = D_MODEL // AT_HEAD_DIM
AT_KV_HEADS = 4
AT_Q_BLOCK = 128
ROPE_THETA = 10000.0

HY_ORDER = 2
HY_SHORT = 3
HY_BANDS = 16
HY_EMB = 1 + 2 * HY_BANDS
HY_FILT_W = 64
HY_FILT_OUT = 2 * (HY_ORDER - 1) * D_MODEL
HY_TARGET = 1e-2
HY_FAST_PCT = 0.3
HY_SLOW_PCT = 1.5
HY_MAX_DECAY = math.log(HY_TARGET) / HY_FAST_PCT
HY_MIN_DECAY = math.log(HY_TARGET) / HY_SLOW_PCT

SSD_INNER = 2 * D_MODEL
SSD_HEAD_DIM = 64
SSD_HEADS = SSD_INNER // SSD_HEAD_DIM
SSD_GROUPS = 4
SSD_STATE = 128
SSD_CONV = 3
SSD_CHUNK = 128
SSD_CONV_DIM = SSD_INNER + 2 * SSD_GROUPS * SSD_STATE
SSD_PROJ = SSD_INNER + SSD_CONV_DIM + 2 * SSD_HEADS

kernel_name = "hybrid_interleaved_dit_trunk"

F32 = jnp.float32


def rms_norm(x, g):
    xf = x.astype(F32)
    y = xf * lax.rsqrt(jnp.mean(xf * xf, axis=-1, keepdims=True) + NORM_EPS)
    return (y * g.astype(F32)).astype(x.dtype)


def layer_norm(x, g, b):
    xf = x.astype(F32)
    mu = jnp.mean(xf, axis=-1, keepdims=True)
    var = jnp.mean(jnp.square(xf - mu), axis=-1, keepdims=True)
    y = (xf - mu) * lax.rsqrt(var + NORM_EPS) * g.astype(F32) + b.astype(F32)
    return y.astype(x.dtype)


def modulate(h, shift, scale):
    return h * (1.0 + scale) + shift


def dwconv_centred(x, w, b):
    K = w.shape[0]
    L = x.shape[1]
    pad = K // 2
    xp = jnp.pad(x, ((0, 0), (pad, pad), (0, 0)))
    return sum(xp[:, k:k + L] * w[k] for k in range(K)) + b


def swiglu(h, w_in, w_out):
    g, u = jnp.split(h @ w_in, 2, axis=-1)
    return (jax.nn.silu(g) * u) @ w_out


def gmlp_mixer(h, w_in, ln_g, ln_b, ws, bs, w_out):
    B_, L, _ = h.shape
    u, v = jnp.split(jax.nn.gelu(h @ w_in, approximate=False), 2, axis=-1)
    v = layer_norm(v, ln_g, ln_b)
    v = v.reshape(B_, L // GM_CHUNK, GM_CHUNK, GM_GROUPS, GM_WIDTH // GM_GROUPS)
    v = jnp.einsum('gpq,bnqgc->bnpgc', ws, v) + bs.T[None, None, :, :, None]
    return (u * v.reshape(B_, L, GM_WIDTH)) @ w_out


def axial_rope(L):
    rows = L // GRID_W
    row = jnp.repeat(jnp.arange(rows, dtype=F32), GRID_W)
    col = jnp.tile(jnp.arange(GRID_W, dtype=F32), rows)
    n = AT_HEAD_DIM // 4
    inv = ROPE_THETA ** (-jnp.arange(n, dtype=F32) / n)
    ang = jnp.concatenate([row[:, None] * inv, col[:, None] * inv], axis=-1)
    return jnp.cos(ang), jnp.sin(ang)


def apply_rope(x, cos, sin):
    half = x.shape[-1] // 2
    x1, x2 = x[..., :half], x[..., half:]
    cs = cos[None, :, None, :].astype(x.dtype)
    sn = sin[None, :, None, :].astype(x.dtype)
    return jnp.concatenate([x1 * cs - x2 * sn, x2 * cs + x1 * sn], axis=-1)


def block_attention(q, k, v):
    B_, Lq = q.shape[:2]
    G = AT_Q_HEADS // AT_KV_HEADS
    nb = Lq // AT_Q_BLOCK
    qb = q.reshape(B_, nb, AT_Q_BLOCK, AT_KV_HEADS, G, AT_HEAD_DIM).transpose(1, 0, 2, 3, 4, 5)
    scale = AT_HEAD_DIM ** -0.5

    def one_block(qi):
        s = jnp.einsum('bqkgd,bskd->bkgqs', qi, k, preferred_element_type=F32) * scale
        p = jax.nn.softmax(s, axis=-1).astype(v.dtype)
        return jnp.einsum('bkgqs,bskd->bqkgd', p, v)

    o = lax.map(one_block, qb)
    return o.transpose(1, 0, 2, 3, 4, 5).reshape(B_, Lq, AT_Q_HEADS, AT_HEAD_DIM)


def attention_mixer(h_lat, h_ctx, w_qkv, q_g, k_g, w_out, want_ctx):
    def project(h):
        B_, L, _ = h.shape
        q, k, v = jnp.split(h @ w_qkv, [AT_Q_HEADS * AT_HEAD_DIM, (AT_Q_HEADS + AT_KV_HEADS) * AT_HEAD_DIM], axis=-1)
        q = rms_norm(q.reshape(B_, L, AT_Q_HEADS, AT_HEAD_DIM), q_g)
        k = rms_norm(k.reshape(B_, L, AT_KV_HEADS, AT_HEAD_DIM), k_g)
        return q, k, v.reshape(B_, L, AT_KV_HEADS, AT_HEAD_DIM)

    B_, L, _ = h_lat.shape
    q_l, k_l, v_l = project(h_lat)
    cos, sin = axial_rope(L)
    q_l, k_l = apply_rope(q_l, cos, sin), apply_rope(k_l, cos, sin)
    q_c, k_c, v_c = project(h_ctx)
    k_all = jnp.concatenate([k_c, k_l], axis=1)
    v_all = jnp.concatenate([v_c, v_l], axis=1)
    y_l = block_attention(q_l, k_all, v_all).reshape(B_, L, AT_Q_HEADS * AT_HEAD_DIM) @ w_out
    y_c = None
    if want_ctx:
        y_c = block_attention(q_c, k_c, v_c).reshape(B_, h_ctx.shape[1], AT_Q_HEADS * AT_HEAD_DIM) @ w_out
    return y_l, y_c


def hyena_filters(L, w1, b1, w2, b2, w3, freq):
    t_idx = jnp.arange(L, dtype=F32)
    t = t_idx / (L - 1)
    w = 2.0 * math.pi * t_idx / L
    f = jnp.linspace(1e-4, HY_BANDS - 1, HY_BANDS, dtype=F32)
    zf = w[:, None] * f[None, :]
    feats = jnp.concatenate([t[:, None], jnp.cos(zf), -jnp.sin(zf)], axis=-1)
    freq = freq.astype(F32)
    hid = jnp.sin(freq * (feats @ w1.astype(F32) + b1.astype(F32)))
    hid = jnp.sin(freq * (hid @ w2.astype(F32) + b2.astype(F32)))
    k = hid @ w3.astype(F32)
    deltas = jnp.abs(jnp.linspace(HY_MIN_DECAY, HY_MAX_DECAY, D_MODEL, dtype=F32))
    decay = jnp.exp(-t[:, None] * jnp.tile(deltas, 2)[None, :])
    return k * decay


def hyena_mixer(h, w_in, conv_w, conv_b, w1, b1, w2, b2, w3, freq, skip, w_out):
    B_, L, D = h.shape
    p = dwconv_centred(h @ w_in, conv_w, conv_b)
    x0, x1, v = jnp.split(p, 3, axis=-1)
    k = hyena_filters(L, w1, b1, w2, b2, w3, freq)
    k_circ = jnp.concatenate([k[:, :D], jnp.zeros((1, D), F32), jnp.flip(k[1:, D:], axis=0)], axis=0)
    u = (x1 * v).astype(F32)
    n = 2 * L
    y = jnp.fft.irfft(jnp.fft.rfft(u, n=n, axis=1) * jnp.fft.rfft(k_circ, n=n, axis=0)[None], n=n, axis=1)[:, :L]
    y = y + u * skip.astype(F32)
    return (x0 * y.astype(h.dtype)) @ w_out


def ssd_scan(x, dt, A, bm, cm, h0, need_y):
    B_, L = x.shape[:2]
    Q, G, R = SSD_CHUNK, SSD_GROUPS, SSD_HEADS // SSD_GROUPS
    nc = L // Q
    xf = x.astype(F32).reshape(B_, nc, Q, G, R, SSD_HEAD_DIM)
    dtf = dt.reshape(B_, nc, Q, G, R)
    bf = bm.astype(F32).reshape(B_, nc, Q, G, SSD_STATE)
    cf = cm.astype(F32).reshape(B_, nc, Q, G, SSD_STATE)
    acs = jnp.cumsum(dtf * A.reshape(G, R), axis=2)
    last = acs[:, :, -1]
    w_end = jnp.exp(last[:, :, None] - acs) * dtf
    states = jnp.einsum('bcjgn,bcjgr,bcjgrp->bcgrpn', bf, w_end, xf)

    def step(h, inp):
        st, dec = inp
        return h * jnp.exp(dec)[..., None, None] + st, h

    h_fin, h_start = lax.scan(step, h0, (jnp.moveaxis(states, 1, 0), jnp.moveaxis(last, 1, 0)))
    if not need_y:
        return None, h_fin
    h_start = jnp.moveaxis(h_start, 0, 1)
    order = jnp.tril(jnp.ones((Q, Q), bool))
    seg = acs[:, :, :, None] - acs[:, :, None, :]
    lmat = jnp.exp(jnp.where(order[:, :, None, None], seg, -jnp.inf))
    cb = jnp.einsum('bcign,bcjgn->bcijg', cf, bf)
    y_diag = jnp.einsum('bcijgr,bcjgrp->bcigrp', cb[..., None] * lmat * dtf[:, :, None], xf)
    y_off = jnp.einsum('bcign,bcgrpn->bcigrp', cf, h_start) * jnp.exp(acs)[..., None]
    return (y_diag + y_off).reshape(B_, L, SSD_HEADS, SSD_HEAD_DIM), h_fin


def ssd_mixer(h_lat, h_ctx, w_in, conv_w, conv_b, a_log, dt_bias, d_skip, norm_g, w_out, want_ctx):
    A = -jnp.exp(a_log.astype(F32))

    def prep(h):
        B_, L, _ = h.shape
        zg, xbc, dt = jnp.split(h @ w_in, [SSD_INNER, SSD_INNER + SSD_CONV_DIM], axis=-1)
        xbc = jax.nn.silu(dwconv_centred(xbc, conv_w, conv_b))
        xs, bm, cm = jnp.split(xbc, [SSD_INNER, SSD_INNER + SSD_GROUPS * SSD_STATE], axis=-1)
        dt = jax.nn.softplus(dt.reshape(B_, L, 2, SSD_HEADS).astype(F32) + dt_bias.astype(F32))
        return (zg, xs.reshape(B_, L, SSD_HEADS, SSD_HEAD_DIM),
                bm.reshape(B_, L, SSD_GROUPS, SSD_STATE), cm.reshape(B_, L, SSD_GROUPS, SSD_STATE), dt)

    z_l, x_l, b_l, c_l, dt_l = prep(h_lat)
    z_c, x_c, b_c, c_c, dt_c = prep(h_ctx)
    B_ = h_lat.shape[0]
    h0 = jnp.zeros((B_, SSD_GROUPS, SSD_HEADS // SSD_GROUPS, SSD_HEAD_DIM, SSD_STATE), F32)
    y_l, y_c = 0.0, 0.0
    for d in range(2):
        fl = (lambda t: jnp.flip(t, axis=1)) if d == 1 else (lambda t: t)
        yc_d, s_ctx = ssd_scan(fl(x_c), fl(dt_c[:, :, d]), A[d], fl(b_c), fl(c_c), h0, want_ctx)
        yl_d, _ = ssd_scan(fl(x_l), fl(dt_l[:, :, d]), A[d], fl(b_l), fl(c_l), s_ctx, True)
        y_l = y_l + fl(yl_d)
        if want_ctx:
            y_c = y_c + fl(yc_d)

    def finish(y, xs, zg):
        B2, L = xs.shape[:2]
        y = (y + xs.astype(F32) * d_skip.astype(F32)[:, None]).reshape(B2, L, SSD_INNER).astype(zg.dtype)
        y = (y * jax.nn.silu(zg)).reshape(B2, L, SSD_GROUPS, SSD_INNER // SSD_GROUPS)
        y = rms_norm(y, norm_g.reshape(SSD_GROUPS, SSD_INNER // SSD_GROUPS)).reshape(B2, L, SSD_INNER)
        return y @ w_out

    out_c = finish(y_c, x_c, z_c) if want_ctx else None
    return finish(y_l, x_l, z_l), out_c


def setup_inputs(seed: int = 0) -> dict:
    key = jax.random.key(seed)
    ks = iter(jax.random.split(key, 64))

    def nrm(shape, scale):
        return jax.random.normal(next(ks), shape, F32) * scale

    def gain(shape):
        return 1.0 + nrm(shape, 0.02)

    D = D_MODEL
    inp = {}
    inp["x"] = nrm((BATCH, SEQ, D), 1.0)
    inp["c"] = nrm((BATCH, D), 1.0)
    inp["ctx"] = nrm((BATCH, CTX_LEN, D), 1.0)
    inp["c_ctx"] = nrm((D,), 1.0)
    inp["norm1_g"] = gain((DEPTH, D))
    inp["norm2_g"] = gain((DEPTH, D))
    inp["mod_w"] = nrm((DEPTH, D, 6 * D), D ** -0.5)
    inp["mod_b"] = nrm((DEPTH, 6 * D), 0.02)
    inp["ffn_w_in"] = nrm((DEPTH, D, 2 * FFN_HIDDEN), D ** -0.5)
    inp["ffn_w_out"] = nrm((DEPTH, FFN_HIDDEN, D), FFN_HIDDEN ** -0.5)
    inp["final_g"] = gain((D,))
    inp["gm_w_in"] = nrm((N_GM, D, 2 * GM_WIDTH), D ** -0.5)
    inp["gm_ln_g"] = gain((N_GM, GM_WIDTH))
    inp["gm_ln_b"] = nrm((N_GM, GM_WIDTH), 0.02)
    inp["gm_ws"] = nrm((N_GM, GM_GROUPS, GM_CHUNK, GM_CHUNK), GM_CHUNK ** -0.5)
    inp["gm_bs"] = gain((N_GM, GM_GROUPS, GM_CHUNK))
    inp["gm_w_out"] = nrm((N_GM, GM_WIDTH, D), GM_WIDTH ** -0.5)
    inp["at_w_qkv"] = nrm((N_AT, D, (AT_Q_HEADS + 2 * AT_KV_HEADS) * AT_HEAD_DIM), D ** -0.5)
    inp["at_q_g"] = gain((N_AT, AT_HEAD_DIM))
    inp["at_k_g"] = gain((N_AT, AT_HEAD_DIM))
    inp["at_w_out"] = nrm((N_AT, AT_Q_HEADS * AT_HEAD_DIM, D), (AT_Q_HEADS * AT_HEAD_DIM) ** -0.5)
    inp["hy_w_in"] = nrm((N_HY, D, (HY_ORDER + 1) * D), D ** -0.5)
    inp["hy_conv_w"] = nrm((N_HY, HY_SHORT, (HY_ORDER + 1) * D), HY_SHORT ** -0.5)
    inp["hy_conv_b"] = nrm((N_HY, (HY_ORDER + 1) * D), 0.02)
    inp["hy_filt_w1"] = nrm((N_HY, HY_EMB, HY_FILT_W), HY_EMB ** -0.5)
    inp["hy_filt_b1"] = nrm((N_HY, HY_FILT_W), 0.1)
    inp["hy_filt_w2"] = nrm((N_HY, HY_FILT_W, HY_FILT_W), HY_FILT_W ** -0.5)
    inp["hy_filt_b2"] = nrm((N_HY, HY_FILT_W), 0.1)
    inp["hy_filt_w3"] = nrm((N_HY, HY_FILT_W, HY_FILT_OUT), 0.004)
    inp["hy_filt_freq"] = gain((N_HY, HY_FILT_W))
    inp["hy_skip"] = nrm((N_HY, D), 0.5)
    inp["hy_w_out"] = nrm((N_HY, D, D), D ** -0.5)
    inp["ssd_w_in"] = nrm((N_SSD, D, SSD_PROJ), D ** -0.5)
    inp["ssd_conv_w"] = nrm((N_SSD, SSD_CONV, SSD_CONV_DIM), SSD_CONV ** -0.5)
    inp["ssd_conv_b"] = nrm((N_SSD, SSD_CONV_DIM), 0.02)
    inp["ssd_a_log"] = jnp.log(jax.random.uniform(next(ks), (N_SSD, 2, SSD_HEADS), F32, 1.0, 16.0))
    dt0 = jnp.exp(jax.random.uniform(next(ks), (N_SSD, 2, SSD_HEADS), F32, math.log(1e-3), math.log(1e-1)))
    inp["ssd_dt_bias"] = dt0 + jnp.log(-jnp.expm1(-dt0))
    inp["ssd_d_skip"] = gain((N_SSD, SSD_HEADS))
    inp["ssd_norm_g"] = gain((N_SSD, SSD_INNER))
    inp["ssd_w_out"] = nrm((N_SSD, SSD_INNER, D), SSD_INNER ** -0.5)
    return inp


def reference(x, c, ctx, c_ctx, norm1_g, norm2_g, mod_w, mod_b, ffn_w_in, ffn_w_out, final_g,
              gm_w_in, gm_ln_g, gm_ln_b, gm_ws, gm_bs, gm_w_out,
              at_w_qkv, at_q_g, at_k_g, at_w_out,
              hy_w_in, hy_conv_w, hy_conv_b, hy_filt_w1, hy_filt_b1, hy_filt_w2, hy_filt_b2,
              hy_filt_w3, hy_filt_freq, hy_skip, hy_w_out,
              ssd_w_in, ssd_conv_w, ssd_conv_b, ssd_a_log, ssd_dt_bias, ssd_d_skip, ssd_norm_g, ssd_w_out):
    z = ctx
    for i in range(DEPTH):
        m, j = i % N_MIXERS, i // N_MIXERS
        want_ctx = i < DEPTH - 1
        ctx_reads = want_ctx or m in (1, 3)
        mod_l = (jax.nn.silu(c) @ mod_w[i] + mod_b[i])[:, None, :]
        mod_c = (jax.nn.silu(c_ctx) @ mod_w[i] + mod_b[i])[None, None, :]
        sh1, sc1, g1, sh2, sc2, g2 = jnp.split(mod_l, 6, axis=-1)
        csh1, csc1, cg1, csh2, csc2, cg2 = jnp.split(mod_c, 6, axis=-1)
        h_l = modulate(rms_norm(x, norm1_g[i]), sh1, sc1)
        h_c = modulate(rms_norm(z, norm1_g[i]), csh1, csc1) if ctx_reads else None
        if m == 0:
            p = (gm_w_in[j], gm_ln_g[j], gm_ln_b[j], gm_ws[j], gm_bs[j], gm_w_out[j])
            y_l = gmlp_mixer(h_l, *p)
            y_c = gmlp_mixer(h_c, *p) if want_ctx else None
        elif m == 1:
            y_l, y_c = attention_mixer(h_l, h_c, at_w_qkv[j], at_q_g[j], at_k_g[j], at_w_out[j], want_ctx)
        elif m == 2:
            p = (hy_w_in[j], hy_conv_w[j], hy_conv_b[j], hy_filt_w1[j], hy_filt_b1[j], hy_filt_w2[j],
                 hy_filt_b2[j], hy_filt_w3[j], hy_filt_freq[j], hy_skip[j], hy_w_out[j])
            y_l = hyena_mixer(h_l, *p)
            y_c = hyena_mixer(h_c, *p) if want_ctx else None
        else:
            y_l, y_c = ssd_mixer(h_l, h_c, ssd_w_in[j], ssd_conv_w[j], ssd_conv_b[j], ssd_a_log[j],
                                 ssd_dt_bias[j], ssd_d_skip[j], ssd_norm_g[j], ssd_w_out[j], want_ctx)
        x = x + g1 * y_l
        x = x + g2 * swiglu(modulate(rms_norm(x, norm2_g[i]), sh2, sc2), ffn_w_in[i], ffn_w_out[i])
        if want_ctx:
            z = z + cg1 * y_c
            z = z + cg2 * swiglu(modulate(rms_norm(z, norm2_g[i]), csh2, csc2), ffn_w_in[i], ffn_w_out[i])
    return rms_norm(x, final_g)
```

```python
import numpy as np
from contextlib import ExitStack
import concourse.bass as bass
import concourse.mybir as mybir
from concourse.bass_utils import run_bass_kernel_spmd

F32 = mybir.dt.float32
BF16 = mybir.dt.bfloat16
AF = mybir.ActivationFunctionType
ALU = mybir.AluOpType
AX = mybir.AxisListType

NCORES = 8
D = 1024
L = 16384
TC = L // NCORES
CTX = 256
DEPTH = 4
FH = 2816
EPS = 1e-6
NDS = 24


class Prog:
    def __init__(self):
        self.nc = bass.Bass("TRN2", target_bir_lowering=False)
        nc = self.nc
        self.es = ExitStack()
        self.E = {'pe': nc.tensor, 'dve': nc.vector, 'act': nc.scalar, 'pool': nc.gpsimd, 'sp': nc.sync}
        self.sems = {}
        self.cnt = {}
        for e in self.E:
            self.sems[e] = self.es.enter_context(nc.semaphore('s_' + e))
            self.cnt[e] = 0
        for i in range(NDS):
            self.sems[('d', i)] = self.es.enter_context(nc.semaphore('d%d' % i))
            self.cnt[('d', i)] = 0
        self.dnext = 0
        self.pfx = ""
        self.bind = {}
        self.internal = set()
        self.made = {}
        self.ext_in = []
        self.ext_out = []
        self.seen = {e: {} for e in self.E}
        self.lastw = {}
        self.readers = {}
        self.ninst = 0

    def sb(self, name, shape, dt):
        return self.es.enter_context(self.nc.sbuf_tensor(self.pfx + name, list(shape), dt))

    def ps(self, name, shape, dt=F32):
        return self.es.enter_context(self.nc.psum_tensor(self.pfx + name, list(shape), dt))

    def dram_in(self, name, shape, dt=F32):
        full = self.pfx + name
        if full in self.bind:
            ap = self.bind[full]
            assert list(ap.shape) == list(shape), (full, ap.shape, shape)
            return ap
        self.ext_in.append(full)
        return self.nc.dram_tensor(full, list(shape), dt, kind="ExternalInput").ap()

    def dram_out(self, name, shape, dt=F32):
        full = self.pfx + name
        if full in self.internal:
            ap = self.nc.dram_tensor(full, list(shape), dt).ap()
        else:
            self.ext_out.append(full)
            ap = self.nc.dram_tensor(full, list(shape), dt, kind="ExternalOutput").ap()
        self.made[full] = ap
        return ap

    def dram_tmp(self, name, shape, dt=F32):
        full = self.pfx + name
        ap = self.nc.dram_tensor(full, list(shape), dt).ap()
        self.made[full] = ap
        return ap

    @staticmethod
    def _k(a):
        if isinstance(a, (str, tuple)):
            return a
        return getattr(a, 'tensor', a).name

    def _wait(self, eng, ev):
        sk, v = ev
        if sk == 'pe' and eng == 'pe':
            return
        if self.seen[eng].get(sk, 0) >= v:
            return
        self.E[eng].wait_ge(self.sems[sk], v)
        self.seen[eng][sk] = v

    def _sync(self, eng, kr, kw, dma_write=False):
        for k in kr:
            for ev in list(self.lastw.get(k, {}).items()):
                self._wait(eng, ev)
        for k in kw:
            for ev in list(self.lastw.get(k, {}).items()):
                if dma_write and isinstance(ev[0], tuple):
                    continue
                self._wait(eng, ev)
            for ev in list(self.readers.get(k, {}).items()):
                self._wait(eng, ev)

    def _record(self, ev, kr, kw, dma_write=False):
        sk, v = ev
        for k in kr:
            self.readers.setdefault(k, {})[sk] = v
        for k in kw:
            if dma_write:
                self.lastw.setdefault(k, {})[sk] = v
            else:
                self.lastw[k] = {sk: v}
                self.readers[k] = {}

    def I(self, eng, meth, *args, r=(), w=(), **kw):
        kr = [self._k(a) for a in r]
        kwr = [self._k(a) for a in w]
        self._sync(eng, kr, kwr)
        ins = getattr(self.E[eng], meth)(*args, **kw)
        self.cnt[eng] += 1
        ins.then_inc(self.sems[eng], 1)
        self._record((eng, self.cnt[eng]), kr, kwr)
        self.ninst += 1
        return ins

    def dma(self, q, out, in_, r=None, w=None):
        kr = [self._k(a) for a in (r if r is not None else [in_])]
        kwr = [self._k(a) for a in (w if w is not None else [out])]
        self._sync(q, kr, kwr, dma_write=True)
        slot = ('d', self.dnext % NDS)
        self.dnext += 1
        if self.cnt[slot] > 0:
            self._wait(q, (slot, self.cnt[slot]))
        ins = self.E[q].dma_start(out=out, in_=in_)
        self.cnt[slot] += 16
        ins.then_inc(self.sems[slot], 16)
        self._record((slot, self.cnt[slot]), kr, kwr, dma_write=True)
        self.ninst += 1

    def finish(self):
        for sk, v in self.cnt.items():
            if v > 0:
                self._wait('sp', (sk, v))

    def mm(self, out, lhsT, rhs, start=True, stop=True):
        self.I('pe', 'matmul', out, lhsT, rhs, start=start, stop=stop, r=[lhsT, rhs], w=[out])

    def tr(self, out, in_, ident):
        self.I('pe', 'transpose', out, in_, ident, r=[in_, ident], w=[out])

    def act(self, out, in_, func, bias=None, scale=None, extra_r=()):
        kw = {}
        r = [in_] + list(extra_r)
        if bias is not None:
            kw['bias'] = bias
            if not isinstance(bias, (int, float)):
                r.append(bias)
        if scale is not None:
            kw['scale'] = scale
            if not isinstance(scale, (int, float)):
                r.append(scale)
        self.I('act', 'activation', out, in_, func, r=r, w=[out], **kw)

    def tt(self, eng, out, in0, in1, op):
        self.I(eng, 'tensor_tensor', out, in0, in1, op, r=[in0, in1], w=[out])

    def ts(self, eng, out, in0, s1, s2, op0, op1=None):
        r = [in0] + [s for s in (s1, s2) if s is not None and not isinstance(s, (int, float))]
        if op1 is None:
            self.I(eng, 'tensor_scalar', out, in0, s1, None, op0, r=r, w=[out])
        else:
            self.I(eng, 'tensor_scalar', out, in0, s1, s2, op0, op1, r=r, w=[out])

    def stt(self, out, in0, scalar, in1, op0, op1):
        r = [in0, in1] + ([] if isinstance(scalar, (int, float)) else [scalar])
        self.I('dve', 'scalar_tensor_tensor', out, in0, scalar, in1, op0, op1, r=r, w=[out])

    def cp(self, eng, out, in_):
        if eng == 'act':
            self.I('act', 'copy', out, in_, r=[in_], w=[out])
        else:
            self.I(eng, 'tensor_copy', out, in_, r=[in_], w=[out])

    def memset(self, eng, ap, v):
        self.I(eng, 'memset', ap, v, r=[], w=[ap])

    def recip(self, out, in_):
        self.I('dve', 'reciprocal', out, in_, r=[in_], w=[out])


def build_mod():
    P = Prog()
    nc = P.nc
    cc = P.dram_in("cc", [128, 8, 2])
    mw = P.dram_in("mod_w", [DEPTH, D, 6 * D])
    mb = P.dram_in("mod_b", [DEPTH, 128, 48])
    out = P.dram_out("modv", [DEPTH, 128, 48, 2])
    cs = P.sb("cs", [128, 8, 2], F32)
    sc = P.sb("sc", [128, 8, 2], F32)
    mbs = P.sb("mbs", [128, DEPTH, 48], F32)
    res = P.sb("res", [128, DEPTH, 48, 2], F32)
    wb = [P.sb("wb%d" % i, [128, 8, 512], F32) for i in range(3)]
    pp = [P.ps("pp%d" % i, [128, 512], F32) for i in range(2)]
    P.dma('sp', cs[:], cc)
    for l in range(DEPTH):
        P.dma('sp', mbs[:, l, :], mb[l])
    P.act(sc[:], cs[:], AF.Silu)
    mwv = mw.rearrange("l (kc p) n -> l p kc n", p=128)
    it = 0
    for l in range(DEPTH):
        for ng in range(12):
            w = wb[it % 3]
            for kc in range(8):
                P.dma('sp' if kc % 2 == 0 else 'act', w[:, kc, :], mwv[l, :, kc, ng * 512:(ng + 1) * 512])
            pt = pp[it % 2]
            for mi in range(4):
                for kc in range(8):
                    P.mm(pt[:, mi * 2:mi * 2 + 2], w[:, kc, mi * 128:(mi + 1) * 128], sc[:, kc, :],
                         start=(kc == 0), stop=(kc == 7))
            m0 = ng * 4
            P.tt('dve', res[:, l, m0:m0 + 4, :], pt[:, 0:8].rearrange("p (m j) -> p m j", j=2),
                 mbs[:, l, m0:m0 + 4].unsqueeze(2).broadcast_to([128, 4, 2]), ALU.add)
            it += 1
    for l in range(DEPTH):
        P.dma('sp', out[l], res[:, l])
    P.finish()
    return P


def run_mod(inputs):
    P = build_mod()
    c = np.asarray(inputs["c"], np.float32).reshape(D)
    cx = np.asarray(inputs["c_ctx"], np.float32).reshape(D)
    cc = np.stack([c.reshape(8, 128).T, cx.reshape(8, 128).T], axis=-1)
    mb = np.asarray(inputs["mod_b"], np.float32).reshape(DEPTH, 48, 128).transpose(0, 2, 1)
    m = {"cc": np.ascontiguousarray(cc), "mod_w": np.ascontiguousarray(inputs["mod_w"], dtype=np.float32),
         "mod_b": np.ascontiguousarray(mb)}
    res = run_bass_kernel_spmd(P.nc, [m], core_ids=[0])
    return res.results[0]["modv"]


def prog_barrier(P):
    cur = [(sk, v) for sk, v in P.cnt.items() if v > 0]
    for e in P.E:
        for ev in cur:
            if ev[0] == e:
                continue
            P._wait(e, ev)
    for e in ('dve', 'act', 'pool'):
        if P.cnt[e] > 0:
            P._wait(e, (e, P.cnt[e]))


class Scope:
    def __init__(self, P):
        self.P = P
        self.es = ExitStack()

    def sb(self, name, shape, dt):
        return self.es.enter_context(self.P.nc.sbuf_tensor(self.P.pfx + name, list(shape), dt))

    def ps(self, name, shape, dt=F32):
        return self.es.enter_context(self.P.nc.psum_tensor(self.P.pfx + name, list(shape), dt))

    def close(self):
        prog_barrier(self.P)
        self.es.close()


class Banks:
    def __init__(self, S, n=8, prefix="pb"):
        self.b = [S.ps("%s%d" % (prefix, i), [128, 512], F32) for i in range(n)]
        self.i = 0

    def next(self):
        b = self.b[self.i % len(self.b)]
        self.i += 1
        return b


def load_w_bf16(P, dst, src_dram, kcs):
    v = src_dram.rearrange("(kc p) n -> p kc n", p=128)
    for kc in range(kcs):
        P.dma('pool', dst[:, kc, :], v[:, kc, :])


def rms_mod(P, S, bk, x, h, gm, sh, j, NT, ones, epsb, tagn):
    sq = S["sq"]
    rs = S["rs"]
    P.tt('pool', sq[:, :, :NT], x[:, :, :NT], x[:, :, :NT], ALU.mult)
    pt = bk.next()
    for kc in range(8):
        P.mm(pt[:, :NT], ones[:, :], sq[:, kc, :NT], start=(kc == 0), stop=(kc == 7))
    P.act(rs[:, :NT], pt[:, :NT], AF.Sqrt, bias=epsb[:, 0:1], scale=1.0 / D)
    P.recip(rs[:, :NT], rs[:, :NT])
    P.tt('dve', sq[:, :, :NT], x[:, :, :NT], rs[:, :NT].unsqueeze(1).broadcast_to([128, 8, NT]), ALU.mult)
    for kc in range(8):
        P.act(h[:, kc, :NT], sq[:, kc, :NT], AF.Identity, bias=sh[:, kc, j:j + 1], scale=gm[:, kc, j:j + 1])


def mod_consts(P, S, modv_d, n1_d, n2_d):
    mv = S.sb("mc_mv", [128, 48, 2], F32)
    n1 = S.sb("mc_n1", [128, 8], F32)
    n2 = S.sb("mc_n2", [128, 8], F32)
    P.dma('sp', mv[:], modv_d)
    P.dma('sp', n1[:], n1_d)
    P.dma('sp', n2[:], n2_d)
    gm1 = S.sb("mc_gm1", [128, 8, 2], F32)
    gm2 = S.sb("mc_gm2", [128, 8, 2], F32)
    for (gm, n, m0) in ((gm1, n1, 8), (gm2, n2, 32)):
        P.ts('dve', gm[:], mv[:, m0:m0 + 8, :], 1.0, None, ALU.add)
        P.tt('dve', gm[:], gm[:], n[:].unsqueeze(2).broadcast_to([128, 8, 2]), ALU.mult)
    return dict(gm1=gm1, sh1=mv[:, 0:8, :], g1=mv[:, 16:24, :], gm2=gm2, sh2=mv[:, 24:32, :], g2=mv[:, 40:48, :])


def ffn_phase(P, tiles, xsrc_of, xdst_of, mc, w_in_d, w_out_d, ones, epsb, NT):
    S = Scope(P)
    wi = S.sb("f_wi", [128, 8, 2 * FH], BF16)
    wo = S.sb("f_wo", [128, 22, D], BF16)
    load_w_bf16(P, wi, w_in_d, 8)
    load_w_bf16(P, wo, w_out_d, 22)
    bk = Banks(S, 8, "fpb")
    xb = [S.sb("f_x%d" % i, [128, 8, NT], F32) for i in range(2)]
    hb = [S.sb("f_h%d" % i, [128, 8, NT], BF16) for i in range(2)]
    ab = [S.sb("f_a%d" % i, [128, 22, NT], BF16) for i in range(2)]
    sgb = [S.sb("f_sg%d" % i, [128, NT], F32) for i in range(3)]
    tmp = {"sq": S.sb("f_sq", [128, 8, NT], F32), "rs": S.sb("f_rs", [128, NT], F32)}
    for ti, (tn, nt, j) in enumerate(tiles):
        x = xb[ti % 2]
        h = hb[ti % 2]
        a = ab[ti % 2]
        src, skey = xsrc_of(tn)
        P.dma('sp', x[:, :, :nt], src, r=[skey])
        rms_mod(P, tmp, bk, x, h, mc["gm2"], mc["sh2"], j, nt, ones, epsb, "f")
        for jj in range(22):
            pg = bk.next()
            pu = bk.next()
            for kc in range(8):
                P.mm(pg[:, :nt], wi[:, kc, jj * 128:(jj + 1) * 128], h[:, kc, :nt], start=(kc == 0), stop=(kc == 7))
            for kc in range(8):
                P.mm(pu[:, :nt], wi[:, kc, FH + jj * 128:FH + (jj + 1) * 128], h[:, kc, :nt],
                     start=(kc == 0), stop=(kc == 7))
            sg = sgb[jj % 3]
            P.act(sg[:, :nt], pg[:, :nt], AF.Silu)
            P.tt('dve', a[:, jj, :nt], sg[:, :nt], pu[:, :nt], ALU.mult)
        for m in range(8):
            po = bk.next()
            for kc in range(22):
                P.mm(po[:, :nt], wo[:, kc, m * 128:(m + 1) * 128], a[:, kc, :nt], start=(kc == 0), stop=(kc == 21))
            P.stt(x[:, m, :nt], po[:, :nt], mc["g2"][:, m, j:j + 1], x[:, m, :nt], ALU.mult, ALU.add)
        dst, dkey = xdst_of(tn)
        P.dma('sp', dst, x[:, :, :nt], w=[dkey])
    S.close()


GW = 2048


def build_l0(NT=256, P=None):
    own = P is None
    P = P if P is not None else Prog()
    xin = P.dram_in("xT", [D, TC])
    zin = P.dram_in("zT", [D, CTX])
    modv = P.dram_in("modv", [128, 48, 2])
    n1d = P.dram_in("n1", [128, 8])
    n2d = P.dram_in("n2", [128, 8])
    gwin = P.dram_in("gm_w_in", [D, 2 * GW])
    gwout = P.dram_in("gm_w_out", [GW, D])
    lng_d = P.dram_in("ln_g", [1, GW])
    lnb_d = P.dram_in("ln_b", [1, GW])
    wsT_d = P.dram_in("wsT", [128, 8, 128])
    bsfc_d = P.dram_in("bsfc", [1, 16 * 128])
    fwin = P.dram_in("ffn_w_in", [D, 2 * FH])
    fwout = P.dram_in("ffn_w_out", [FH, D])
    xout = P.dram_out("xT_out", [D, TC])
    zout = P.dram_out("zT_out", [D, CTX])

    G = Scope(P)
    ones = G.sb("ones", [128, 128], F32)
    epsb = G.sb("epsb", [128, 1], F32)
    P.memset('dve', ones[:], 1.0)
    P.memset('dve', epsb[:], EPS)
    mc = mod_consts(P, G, modv, n1d, n2d)

    xin_v = xin.rearrange("(kc p) t -> p kc t", p=128)
    zin_v = zin.rearrange("(kc p) t -> p kc t", p=128)
    xout_v = xout.rearrange("(kc p) t -> p kc t", p=128)
    zout_v = zout.rearrange("(kc p) t -> p kc t", p=128)
    tiles = [(("x", i), NT, 0) for i in range(TC // NT)] + [(("z", i), NT, 1) for i in range(CTX // NT)]

    def src_in(tn):
        v = xin_v if tn[0] == "x" else zin_v
        return v[:, :, tn[1] * NT:(tn[1] + 1) * NT], ("in",) + tn

    def dst_out(tn):
        v = xout_v if tn[0] == "x" else zout_v
        return v[:, :, tn[1] * NT:(tn[1] + 1) * NT], ("out",) + tn

    S = Scope(P)
    wu = S.sb("g_wu", [128, 8, GW], BF16)
    wv = S.sb("g_wv", [128, 8, GW], BF16)
    wo = S.sb("g_wo", [128, 16, D], BF16)
    gv = gwin.rearrange("(kc p) n -> p kc n", p=128)
    for kc in range(8):
        P.dma('pool', wu[:, kc, :], gv[:, kc, 0:GW])
        P.dma('pool', wv[:, kc, :], gv[:, kc, GW:2 * GW])
    load_w_bf16(P, wo, gwout, 16)
    wsT = S.sb("g_wsT", [128, 8, 128], BF16)
    P.dma('pool', wsT[:], wsT_d)
    lng = S.sb("g_lng", [128, GW], F32)
    lnb = S.sb("g_lnb", [128, GW], F32)
    bsfc = S.sb("g_bsfc", [128, 16, 128], F32)
    P.dma('sp', lng[:], lng_d.broadcast_to([128, GW]))
    P.dma('sp', lnb[:], lnb_d.broadcast_to([128, GW]))
    P.dma('sp', bsfc[:].rearrange("p a b -> p (a b)"), bsfc_d.broadcast_to([128, 16 * 128]))
    bk = Banks(S, 8, "gpb")
    xb = [S.sb("g_x%d" % i, [128, 8, NT], F32) for i in range(2)]
    h = S.sb("g_h", [128, 8, NT], BF16)
    u = S.sb("g_u", [128, 16, NT], BF16)
    gated = S.sb("g_gated", [128, 16, NT], BF16)
    vt = S.sb("g_vt", [128, GW], F32)
    vnb = S.sb("g_vnb", [128, GW], BF16)
    st = S.sb("g_st", [128, 4, 6], F32)
    mvv = S.sb("g_mvv", [128, 2], F32)
    rstd = S.sb("g_rstd", [128, 1], F32)
    gtmp = [S.sb("g_gtmp%d" % i, [128, 4, 128], F32) for i in range(2)]
    tmp = {"sq": S.sb("g_sq", [128, 8, NT], F32), "rs": S.sb("g_rs", [128, NT], F32)}
    for ti, (tn, nt, j) in enumerate(tiles):
        x = xb[ti % 2]
        src, skey = src_in(tn)
        P.dma('sp', x[:, :, :nt], src, r=[skey])
        rms_mod(P, tmp, bk, x, h, mc["gm1"], mc["sh1"], j, nt, ones, epsb, "g")
        for m in range(16):
            pt = bk.next()
            for kc in range(8):
                P.mm(pt[:, :nt], wu[:, kc, m * 128:(m + 1) * 128], h[:, kc, :nt], start=(kc == 0), stop=(kc == 7))
            P.act(u[:, m, :nt], pt[:, :nt], AF.Gelu)
        for c in range(nt // 128):
            for n in range(4):
                pt = bk.next()
                for kc in range(8):
                    P.mm(pt[:, :], h[:, kc, c * 128:(c + 1) * 128], wv[:, kc, n * 512:(n + 1) * 512],
                         start=(kc == 0), stop=(kc == 7))
                P.act(vt[:, n * 512:(n + 1) * 512], pt[:, :], AF.Gelu)
                P.I('dve', 'bn_stats', st[:, n, :], vt[:, n * 512:(n + 1) * 512], r=[vt], w=[st])
            P.I('dve', 'bn_aggr', mvv[:], st[:].rearrange("p a b -> p (a b)"), r=[st], w=[mvv])
            P.act(rstd[:], mvv[:, 1:2], AF.Sqrt, bias=epsb[:, 0:1], scale=1.0)
            P.recip(rstd[:], rstd[:])
            P.ts('dve', vt[:], vt[:], mvv[:, 0:1], rstd[:, 0:1], ALU.subtract, ALU.mult)
            P.tt('pool', vt[:], vt[:], lng[:], ALU.mult)
            P.tt('dve', vnb[:], vt[:], lnb[:], ALU.add)
            for fb in range(4):
                pg = bk.next()
                for i in range(4):
                    fc = fb * 4 + i
                    P.mm(pg[:, i * 128:(i + 1) * 128], vnb[:, fc * 128:(fc + 1) * 128], wsT[:, fc // 2, :],
                         start=True, stop=True)
                gt = gtmp[fb % 2]
                P.tt('dve', gt[:], pg[:, :].rearrange("p (a b) -> p a b", b=128), bsfc[:, fb * 4:(fb + 1) * 4, :], ALU.add)
                P.tt('pool', gated[:, fb * 4:(fb + 1) * 4, c * 128:(c + 1) * 128], gt[:],
                     u[:, fb * 4:(fb + 1) * 4, c * 128:(c + 1) * 128], ALU.mult)
        for m in range(8):
            po = bk.next()
            for kc in range(16):
                P.mm(po[:, :nt], wo[:, kc, m * 128:(m + 1) * 128], gated[:, kc, :nt], start=(kc == 0), stop=(kc == 15))
            P.stt(x[:, m, :nt], po[:, :nt], mc["g1"][:, m, j:j + 1], x[:, m, :nt], ALU.mult, ALU.add)
        dst, dkey = dst_out(tn)
        P.dma('sp', dst, x[:, :, :nt], w=[dkey])
    S.close()

    def src_mid(tn):
        d, k = dst_out(tn)
        return d, k
    ffn_phase(P, tiles, src_mid, dst_out, mc, fwin, fwout, ones, epsb, NT)
    G.close()
    if own:
        P.finish()
    return P


def lay_vec8(v):
    return np.ascontiguousarray(np.asarray(v, np.float32).reshape(8, 128).T)


def run_l0(inputs, modv, xT_shards, zT):
    P = build_l0()
    ws = np.asarray(inputs["gm_ws"], np.float32)[0]
    wsT = np.ascontiguousarray(ws.transpose(2, 0, 1))
    bs = np.asarray(inputs["gm_bs"], np.float32)[0]
    bsfc = np.ascontiguousarray(np.repeat(bs, 2, axis=0).reshape(1, 16 * 128))
    common = {
        "zT": zT, "modv": np.ascontiguousarray(modv[0]),
        "n1": lay_vec8(inputs["norm1_g"][0]), "n2": lay_vec8(inputs["norm2_g"][0]),
        "gm_w_in": np.ascontiguousarray(inputs["gm_w_in"][0], dtype=np.float32),
        "gm_w_out": np.ascontiguousarray(inputs["gm_w_out"][0], dtype=np.float32),
        "ln_g": np.ascontiguousarray(inputs["gm_ln_g"][0].reshape(1, GW), dtype=np.float32),
        "ln_b": np.ascontiguousarray(inputs["gm_ln_b"][0].reshape(1, GW), dtype=np.float32),
        "wsT": wsT, "bsfc": bsfc,
        "ffn_w_in": np.ascontiguousarray(inputs["ffn_w_in"][0], dtype=np.float32),
        "ffn_w_out": np.ascontiguousarray(inputs["ffn_w_out"][0], dtype=np.float32),
    }
    maps = [dict(common, xT=xT_shards[i]) for i in range(NCORES)]
    res = run_bass_kernel_spmd(P.nc, maps, core_ids=list(range(NCORES)))
    return [r["xT_out"] for r in res.results], res.results[0]["zT_out"]


HD = 64
NQH = 16
NKV = 4


def rope_tables(t0, n):
    t = np.arange(t0, t0 + n)
    row = (t // 64).astype(np.float32)
    col = (t % 64).astype(np.float32)
    nf = HD // 4
    inv = (np.float32(10000.0) ** (-np.arange(nf, dtype=np.float32) / np.float32(nf))).astype(np.float32)
    ang = np.concatenate([row[:, None] * inv, col[:, None] * inv], axis=-1).astype(np.float32)
    c = np.cos(ang).astype(np.float32)
    s = np.sin(ang).astype(np.float32)
    return (np.ascontiguousarray(np.concatenate([c, c], axis=1).T), np.ascontiguousarray(np.concatenate([s, s], axis=1).T))


def rot_matrix():
    r = np.zeros((64, 64), np.float32)
    for m in range(32):
        r[m + 32, m] = -1.0
    for m in range(32, 64):
        r[m - 32, m] = 1.0
    return r


def build_l1a(NT=256, P=None):
    own = P is None
    P = P if P is not None else Prog()
    xin = P.dram_in("xT", [D, TC])
    zin = P.dram_in("zT", [D, CTX])
    modv = P.dram_in("modv", [128, 48, 2])
    n1d = P.dram_in("n1", [128, 8])
    n2d = P.dram_in("n2", [128, 8])
    wqkv = P.dram_in("w_qkv", [D, 1536])
    qg_d = P.dram_in("qg", [64, 1])
    kg_d = P.dram_in("kg", [64, 1])
    cos_d = P.dram_in("cosT", [64, TC])
    sin_d = P.dram_in("sinT", [64, TC])
    rot_d = P.dram_in("rotm", [64, 64])
    QT = P.dram_out("QT", [64, NQH, TC])
    KT = P.dram_out("KT", [64, NKV, TC])
    VT = P.dram_out("Vtok", [TC, 256])
    QTc = P.dram_out("QTc", [64, NQH, CTX])
    KTc = P.dram_out("KTc", [64, NKV, CTX])
    VTc = P.dram_out("Vtokc", [CTX, 256])

    G = Scope(P)
    ones = G.sb("ones", [128, 128], F32)
    epsb = G.sb("epsb", [128, 1], F32)
    P.memset('dve', ones[:], 1.0)
    P.memset('dve', epsb[:], EPS)
    mc = mod_consts(P, G, modv, n1d, n2d)
    w = G.sb("a_w", [128, 8, 1536], BF16)
    load_w_bf16(P, w, wqkv, 8)
    gq = G.sb("a_gq", [64, 2], F32)
    P.dma('sp', gq[:, 0:1], qg_d)
    P.dma('sp', gq[:, 1:2], kg_d)
    cosT = G.sb("a_cos", [64, TC], F32)
    sinT = G.sb("a_sin", [64, TC], F32)
    rot = G.sb("a_rot", [64, 64], F32)
    P.dma('sp', cosT[:], cos_d)
    P.dma('sp', sinT[:], sin_d)
    P.dma('sp', rot[:], rot_d)
    bk = Banks(G, 8, "apb")
    xb = [G.sb("a_x%d" % i, [128, 8, NT], F32) for i in range(2)]
    h = G.sb("a_h", [128, 8, NT], BF16)
    tmp = {"sq": G.sb("a_sq", [128, 8, NT], F32), "rs": G.sb("a_rs", [128, NT], F32)}
    qsq = [G.sb("a_qsq%d" % i, [64, NT], F32) for i in range(2)]
    qrs = [G.sb("a_qrs%d" % i, [64, NT], F32) for i in range(2)]
    qn = [G.sb("a_qn%d" % i, [64, NT], F32) for i in range(2)]
    qa = [G.sb("a_qa%d" % i, [64, NT], F32) for i in range(2)]
    qb_ = [G.sb("a_qb%d" % i, [64, NT], F32) for i in range(2)]
    qo = G.sb("a_qo", [64, NQH + NKV, NT], F32)
    vs = [G.sb("a_vs%d" % i, [128, 256], F32) for i in range(2)]

    xin_v = xin.rearrange("(kc p) t -> p kc t", p=128)
    zin_v = zin.rearrange("(kc p) t -> p kc t", p=128)
    tiles = [("x", i, 0) for i in range(TC // NT)] + [("z", i, 1) for i in range(CTX // NT)]
    for ti, (kind, i, j) in enumerate(tiles):
        x = xb[ti % 2]
        t0 = i * NT
        src = (xin_v if kind == "x" else zin_v)[:, :, t0:t0 + NT]
        P.dma('sp', x[:], src)
        rms_mod(P, tmp, bk, x, h, mc["gm1"], mc["sh1"], j, NT, ones, epsb, "a")
        for hh in range(NQH + NKV):
            b = hh % 2
            pq = bk.next()
            for kc in range(8):
                P.mm(pq[0:64, :NT], w[:, kc, hh * 64:(hh + 1) * 64], h[:, kc, :], start=(kc == 0), stop=(kc == 7))
            P.act(qsq[b][:], pq[0:64, :NT], AF.Square)
            p2 = bk.next()
            P.mm(p2[0:64, :NT], ones[0:64, 0:64], qsq[b][:], start=True, stop=True)
            P.act(qrs[b][:], p2[0:64, :NT], AF.Sqrt, bias=epsb[0:64, 0:1], scale=1.0 / HD)
            P.recip(qrs[b][:], qrs[b][:])
            gcol = gq[:, 0:1] if hh < NQH else gq[:, 1:2]
            if kind == "x":
                P.stt(qn[b][:], pq[0:64, :NT], gcol, qrs[b][:], ALU.mult, ALU.mult)
                p3 = bk.next()
                P.mm(p3[0:64, :NT], rot[:, :], qn[b][:], start=True, stop=True)
                P.tt('pool', qa[b][:], qn[b][:], cosT[:, t0:t0 + NT], ALU.mult)
                P.tt('dve', qb_[b][:], p3[0:64, :NT], sinT[:, t0:t0 + NT], ALU.mult)
                P.tt('pool', qo[:, hh, :], qa[b][:], qb_[b][:], ALU.add)
            else:
                P.stt(qo[:, hh, :], pq[0:64, :NT], gcol, qrs[b][:], ALU.mult, ALU.mult)
        qdst = (QT if kind == "x" else QTc)[:, :, t0:t0 + NT]
        kdst = (KT if kind == "x" else KTc)[:, :, t0:t0 + NT]
        P.dma('sp', qdst, qo[:, 0:NQH, :])
        P.dma('sp', kdst, qo[:, NQH:NQH + NKV, :])
        for c in range(NT // 128):
            pv = bk.next()
            for kc in range(8):
                P.mm(pv[:, 0:256], h[:, kc, c * 128:(c + 1) * 128], w[:, kc, 1280:1536], start=(kc == 0), stop=(kc == 7))
            v = vs[c % 2]
            P.cp('act', v[:], pv[:, 0:256])
            vdst = (VT if kind == "x" else VTc)[t0 + c * 128:t0 + (c + 1) * 128, :]
            P.dma('sp', vdst, v[:])
    G.close()
    if own:
        P.finish()
    return P


def run_l1a(inputs, modv, xT_shards, zT):
    P = build_l1a()
    common = {
        "zT": zT, "modv": np.ascontiguousarray(modv[1]),
        "n1": lay_vec8(inputs["norm1_g"][1]), "n2": lay_vec8(inputs["norm2_g"][1]),
        "w_qkv": np.ascontiguousarray(inputs["at_w_qkv"][0], dtype=np.float32),
        "qg": np.ascontiguousarray(np.asarray(inputs["at_q_g"][0], np.float32).reshape(64, 1)),
        "kg": np.ascontiguousarray(np.asarray(inputs["at_k_g"][0], np.float32).reshape(64, 1)),
        "rotm": rot_matrix(),
    }
    maps = []
    for i in range(NCORES):
        c, s = rope_tables(i * TC, TC)
        maps.append(dict(common, xT=xT_shards[i], cosT=c, sinT=s))
    res = run_bass_kernel_spmd(P.nc, maps, core_ids=list(range(NCORES)))
    return res.results


NK = CTX + L
NSB = NK // 128


def attn_shift(P, S, bk, qg_row_d, kg_row_d, ones):
    g2 = S.sb("s_g2", [1, 2, 64], F32)
    P.dma('sp', g2[:, 0, :], qg_row_d)
    P.dma('sp', g2[:, 1, :], kg_row_d)
    mx = S.sb("s_mx", [1, 2], F32)
    P.I('dve', 'tensor_reduce', mx[:], g2[:], AX.X, ALU.max, apply_absolute_value=True, r=[g2], w=[mx])
    pr = S.sb("s_pr", [1, 1], F32)
    P.tt('dve', pr[:], mx[:, 0:1], mx[:, 1:2], ALU.mult)
    pt = bk.next()
    P.mm(pt[:, 0:1], ones[0:1, :], pr[:], start=True, stop=True)
    ns = S.sb("s_ns", [128, 1], F32)
    P.ts('dve', ns[:], pt[:, 0:1], -8.0, None, ALU.mult)
    return ns


def build_l1b(NT=256, P=None):
    own = P is None
    P = P if P is not None else Prog()
    QT = P.dram_in("QT", [64, NQH, TC])
    QTc = P.dram_in("QTc", [64, NQH, CTX])
    KTall = P.dram_in("KTall", [64, NKV, NK])
    Vr = P.dram_in("Vr", [NKV, 128, NSB, 64])
    qg_row = P.dram_in("qg_row", [1, 64])
    kg_row = P.dram_in("kg_row", [1, 64])
    xin = P.dram_in("xT", [D, TC])
    zin = P.dram_in("zT", [D, CTX])
    modv = P.dram_in("modv", [128, 48, 2])
    n1d = P.dram_in("n1", [128, 8])
    n2d = P.dram_in("n2", [128, 8])
    awo = P.dram_in("at_w_out", [D, D])
    fwin = P.dram_in("ffn_w_in", [D, 2 * FH])
    fwout = P.dram_in("ffn_w_out", [FH, D])
    xout = P.dram_out("xT_out", [D, TC])
    zout = P.dram_out("zT_out", [D, CTX])

    G = Scope(P)
    ones = G.sb("ones", [128, 128], F32)
    epsb = G.sb("epsb", [128, 1], F32)
    P.memset('dve', ones[:], 1.0)
    P.memset('dve', epsb[:], EPS)
    mc = mod_consts(P, G, modv, n1d, n2d)

    AT = Scope(P)
    attnT = AT.sb("t_attnT", [64, NQH, TC], BF16)
    attnTc = AT.sb("t_attnTc", [64, NQH, CTX], BF16)

    S = Scope(P)
    bk = Banks(S, 6, "spb")
    accb = [S.ps("acc%d" % i, [128, 512], F32) for i in range(2)]
    ns = attn_shift(P, S, bk, qg_row, kg_row, ones)
    KTg = S.sb("t_K", [64, NK], BF16)
    Vp = S.sb("t_V", [128, NSB, 128], BF16)
    Qg = S.sb("t_Q", [64, 4, TC], BF16)
    Qc = S.sb("t_Qc", [64, 4, CTX], BF16)
    pTb = [S.sb("t_p%d" % i, [128, 512], BF16) for i in range(4)]
    osb = [S.sb("t_o%d" % i, [128, 512], F32) for i in range(2)]
    rsb = [S.sb("t_r%d" % i, [64, 512], F32) for i in range(2)]
    it = 0
    nacc = 0
    for g in range(NKV):
        P.dma('pool', KTg[:, :], KTall[:, g, :])
        P.memset('pool', Vp[:], 1.0)
        for b0 in range(0, NSB, 13):
            P.dma('pool', Vp[:, b0:b0 + 13, 0:64], Vr[g, :, b0:b0 + 13, :])
        for r in range(4):
            P.dma('pool', Qg[:, r, :], QT[:, g * 4 + r, :])
            P.dma('pool', Qc[:, r, :], QTc[:, g * 4 + r, :])
        for r in range(4):
            hh = g * 4 + r
            jobs = [(Qg[:, r, qb * 512:(qb + 1) * 512], 512, NSB, attnT[:, hh, qb * 512:(qb + 1) * 512]) for qb in range(TC // 512)]
            jobs.append((Qc[:, r, :], CTX, CTX // 128, attnTc[:, hh, :]))
            for (qap, nq, nsb, dst) in jobs:
                acc = accb[nacc % 2]
                for sb in range(nsb):
                    ps = bk.next()
                    P.mm(ps[:, :nq], KTg[:, sb * 128:(sb + 1) * 128], qap, start=True, stop=True)
                    pT = pTb[it % 4]
                    it += 1
                    P.act(pT[:, :nq], ps[:, :nq], AF.Exp, bias=ns[:, 0:1], scale=0.125)
                    P.mm(acc[:, :nq], Vp[:, sb, :], pT[:, :nq], start=(sb == 0), stop=(sb == nsb - 1))
                o = osb[nacc % 2]
                rr = rsb[nacc % 2]
                nacc += 1
                P.cp('dve', o[:, :nq], acc[:, :nq])
                P.dma('sp', rr[:, :nq], o[64:128, :nq])
                P.recip(rr[:, :nq], rr[:, :nq])
                P.tt('pool', dst, o[0:64, :nq], rr[:, :nq], ALU.mult)
    S.close()

    xin_v = xin.rearrange("(kc p) t -> p kc t", p=128)
    zin_v = zin.rearrange("(kc p) t -> p kc t", p=128)
    xout_v = xout.rearrange("(kc p) t -> p kc t", p=128)
    zout_v = zout.rearrange("(kc p) t -> p kc t", p=128)
    tiles = [(("x", i), NT, 0) for i in range(TC // NT)] + [(("z", i), NT, 1) for i in range(CTX // NT)]

    def dst_out(tn):
        v = xout_v if tn[0] == "x" else zout_v
        return v[:, :, tn[1] * NT:(tn[1] + 1) * NT], ("out",) + tn

    S = Scope(P)
    bk = Banks(S, 8, "opb")
    wo = S.sb("o_wo", [64, NQH, D], BF16)
    P.dma('pool', wo[:], awo.rearrange("(h d) n -> d h n", d=64))
    xb = [S.sb("o_x%d" % i, [128, 8, NT], F32) for i in range(2)]
    for ti, (tn, nt, j) in enumerate(tiles):
        x = xb[ti % 2]
        t0 = tn[1] * NT
        P.dma('sp', x[:], (xin_v if tn[0] == "x" else zin_v)[:, :, t0:t0 + NT])
        a = attnT if tn[0] == "x" else attnTc
        for m in range(8):
            po = bk.next()
            for hh in range(NQH):
                P.mm(po[:, :nt], wo[:, hh, m * 128:(m + 1) * 128], a[:, hh, t0:t0 + NT], start=(hh == 0), stop=(hh == NQH - 1))
            P.stt(x[:, m, :], po[:, :nt], mc["g1"][:, m, j:j + 1], x[:, m, :], ALU.mult, ALU.add)
        dst, dkey = dst_out(tn)
        P.dma('sp', dst, x[:], w=[dkey])
    S.close()
    AT.close()

    ffn_phase(P, tiles, dst_out, dst_out, mc, fwin, fwout, ones, epsb, NT)
    G.close()
    if own:
        P.finish()
    return P


def run_l1b(inputs, modv, xT_shards, zT, a_res):
    P = build_l1b()
    KTall = np.ascontiguousarray(np.concatenate([a_res[0]["KTc"]] + [r["KT"] for r in a_res], axis=2))
    Vall = np.concatenate([a_res[0]["Vtokc"]] + [r["Vtok"] for r in a_res], axis=0)
    Vr = np.ascontiguousarray(Vall.reshape(NSB, 128, NKV, 64).transpose(2, 1, 0, 3))
    common = {
        "QTc": a_res[0]["QTc"], "KTall": KTall, "Vr": Vr,
        "qg_row": np.ascontiguousarray(np.asarray(inputs["at_q_g"][0], np.float32).reshape(1, 64)),
        "kg_row": np.ascontiguousarray(np.asarray(inputs["at_k_g"][0], np.float32).reshape(1, 64)),
        "zT": zT, "modv": np.ascontiguousarray(modv[1]),
        "n1": lay_vec8(inputs["norm1_g"][1]), "n2": lay_vec8(inputs["norm2_g"][1]),
        "at_w_out": np.ascontiguousarray(inputs["at_w_out"][0], dtype=np.float32),
        "ffn_w_in": np.ascontiguousarray(inputs["ffn_w_in"][1], dtype=np.float32),
        "ffn_w_out": np.ascontiguousarray(inputs["ffn_w_out"][1], dtype=np.float32),
    }
    maps = [dict(common, QT=a_res[i]["QT"], xT=xT_shards[i]) for i in range(NCORES)]
    res = run_bass_kernel_spmd(P.nc, maps, core_ids=list(range(NCORES)))
    return [r["xT_out"] for r in res.results], res.results[0]["zT_out"]


HY_EMB = 33
HY_W = 64
NFFT = 2 * L


def hyena_pos_tables(Lx):
    t_idx = np.arange(Lx, dtype=np.float32)
    t = (t_idx / np.float32(Lx - 1)).astype(np.float32)
    w = (np.float32(2.0 * np.pi) * t_idx / np.float32(Lx)).astype(np.float32)
    f = np.linspace(1e-4, 15, 16, dtype=np.float32)
    zf = (w[:, None] * f[None, :]).astype(np.float32)
    feats = np.concatenate([t[:, None], np.cos(zf), -np.sin(zf)], axis=-1).astype(np.float32)
    fT = np.zeros((HY_W, Lx), np.float32)
    fT[:HY_EMB] = feats.T
    return fT, np.ascontiguousarray(t.reshape(1, Lx))


def hyena_deltas():
    mx = np.log(1e-2) / 0.3
    mn = np.log(1e-2) / 1.5
    return np.abs(np.linspace(mn, mx, D, dtype=np.float32)).astype(np.float32)


def build_l2f(P=None):
    own = P is None
    P = P if P is not None else Prog()
    fe_l = P.dram_in("feT_l", [HY_W, L])
    tn_l = P.dram_in("tn_l", [1, L])
    fe_c = P.dram_in("feT_c", [HY_W, CTX])
    tn_c = P.dram_in("tn_c", [1, CTX])
    w1d = P.dram_in("w1", [HY_W, HY_W])
    w2d = P.dram_in("w2", [HY_W, HY_W])
    w3fd = P.dram_in("w3f", [HY_W, 128])
    w3bd = P.dram_in("w3b", [HY_W, 128])
    vecd = P.dram_in("vecs", [HY_W, 3])
    ndd = P.dram_in("ndelta", [128, 1])
    kf_l = P.dram_out("kf_l", [128, L])
    kb_l = P.dram_out("kb_l", [128, L])
    kf_c = P.dram_out("kf_c", [128, CTX])
    kb_c = P.dram_out("kb_c", [128, CTX])
    G = Scope(P)
    w1 = G.sb("f_w1", [HY_W, HY_W], F32)
    w2 = G.sb("f_w2", [HY_W, HY_W], F32)
    w3f = G.sb("f_w3f", [HY_W, 128], F32)
    w3b = G.sb("f_w3b", [HY_W, 128], F32)
    vec = G.sb("f_vec", [HY_W, 3], F32)
    nd = G.sb("f_nd", [128, 1], F32)
    for (dst, src) in ((w1, w1d), (w2, w2d), (w3f, w3fd), (w3b, w3bd), (vec, vecd), (nd, ndd)):
        P.dma('sp', dst[:], src)
    bk = Banks(G, 8, "fpb")
    NTF = 512
    fe = [G.sb("f_fe%d" % i, [HY_W, NTF], F32) for i in range(2)]
    tn = [G.sb("f_tn%d" % i, [128, NTF], F32) for i in range(2)]
    arg = [G.sb("f_arg%d" % i, [HY_W, NTF], F32) for i in range(2)]
    h1 = G.sb("f_h1", [HY_W, NTF], F32)
    h2 = G.sb("f_h2", [HY_W, NTF], F32)
    dec = G.sb("f_dec", [128, NTF], F32)
    of = [G.sb("f_of%d" % i, [128, NTF], F32) for i in range(2)]
    ob = [G.sb("f_ob%d" % i, [128, NTF], F32) for i in range(2)]
    PI = float(np.pi)

    fr2 = G.sb("f_fr2", [HY_W, 1], F32)
    P.ts('dve', fr2[:], vec[:, 2:3], 1.0 / (2.0 * PI), None, ALU.mult)

    def sin_layer(dst, ps, bcol, nt, a):
        P.ts('dve', a[:, :nt], ps, vec[:, bcol:bcol + 1], fr2[:, 0:1], ALU.add, ALU.mult)
        for _ in range(4):
            P.stt(a[:, :nt], a[:, :nt], 0.5, a[:, :nt], ALU.is_gt, ALU.subtract)
        P.act(dst[:, :nt], a[:, :nt], AF.Sin, scale=2.0 * PI * (1.0 - 1e-6))

    it = 0
    for (fed, tnd, Lx, kfo, kbo) in ((fe_l, tn_l, L, kf_l, kb_l), (fe_c, tn_c, CTX, kf_c, kb_c)):
        nt = min(NTF, Lx)
        for t0 in range(0, Lx, nt):
            b = it % 2
            it += 1
            P.dma('sp', fe[b][:, :nt], fed[:, t0:t0 + nt])
            P.dma('sp', tn[b][:, :nt], tnd[:, t0:t0 + nt].broadcast_to([128, nt]))
            p1 = bk.next()
            P.mm(p1[0:HY_W, :nt], w1[:, :], fe[b][:, :nt], start=True, stop=True)
            sin_layer(h1, p1[0:HY_W, :nt], 0, nt, arg[0])
            p2 = bk.next()
            P.mm(p2[0:HY_W, :nt], w2[:, :], h1[:, :nt], start=True, stop=True)
            sin_layer(h2, p2[0:HY_W, :nt], 1, nt, arg[1])
            pf = bk.next()
            pb = bk.next()
            P.mm(pf[:, :nt], w3f[:, :], h2[:, :nt], start=True, stop=True)
            P.mm(pb[:, :nt], w3b[:, :], h2[:, :nt], start=True, stop=True)
            P.act(dec[:, :nt], tn[b][:, :nt], AF.Exp, scale=nd[:, 0:1])
            P.tt('dve', of[b][:, :nt], pf[:, :nt], dec[:, :nt], ALU.mult)
            P.tt('dve', ob[b][:, :nt], pb[:, :nt], dec[:, :nt], ALU.mult)
            if t0 == 0:
                P.memset('dve', ob[b][:, 0:1], 0.0)
            P.dma('sp', kfo[:, t0:t0 + nt], of[b][:, :nt])
            P.dma('sp', kbo[:, t0:t0 + nt], ob[b][:, :nt])
    G.close()
    if own:
        P.finish()
    return P


def run_l2f(inputs):
    P = build_l2f()
    feT_l, tn_l = hyena_pos_tables(L)
    feT_c, tn_c = hyena_pos_tables(CTX)
    w3 = np.asarray(inputs["hy_filt_w3"][0], np.float32)
    vecs = np.stack([inputs["hy_filt_b1"][0], inputs["hy_filt_b2"][0], inputs["hy_filt_freq"][0]], axis=1).astype(np.float32)
    dl = hyena_deltas()
    common = {"feT_l": feT_l, "tn_l": tn_l, "feT_c": feT_c, "tn_c": tn_c,
              "w1": np.ascontiguousarray(np.concatenate([np.asarray(inputs["hy_filt_w1"][0], np.float32), np.zeros((HY_W - HY_EMB, HY_W), np.float32)], axis=0)),
              "w2": np.ascontiguousarray(inputs["hy_filt_w2"][0], dtype=np.float32),
              "vecs": np.ascontiguousarray(vecs)}
    maps = []
    for c in range(NCORES):
        sl = slice(c * 128, (c + 1) * 128)
        maps.append(dict(common, w3f=np.ascontiguousarray(w3[:, sl]), w3b=np.ascontiguousarray(w3[:, D + c * 128:D + (c + 1) * 128]),
                         ndelta=np.ascontiguousarray(-dl[sl].reshape(128, 1))))
    res = run_bass_kernel_spmd(P.nc, maps, core_ids=list(range(NCORES)))
    return res.results


def build_l2a(NT=256, P=None):
    own = P is None
    P = P if P is not None else Prog()
    NH = NT + 2
    xin = P.dram_in("xh", [D, TC + 2])
    zin = P.dram_in("zh", [D, CTX + 2])
    hmd = P.dram_in("hmask", [128, 2])
    modv = P.dram_in("modv", [128, 48, 2])
    n1d = P.dram_in("n1", [128, 8])
    n2d = P.dram_in("n2", [128, 8])
    wind = P.dram_in("hy_w_in", [D, 3 * D])
    cwd = P.dram_in("cw", [128, 3, 24])
    cbd = P.dram_in("cb", [128, 24])
    uT = P.dram_out("uT", [D, TC])
    x0T = P.dram_out("x0T", [D, TC])
    uTc = P.dram_out("uTc", [D, CTX])
    x0Tc = P.dram_out("x0Tc", [D, CTX])
    G = Scope(P)
    ones = G.sb("ones", [128, 128], F32)
    epsb = G.sb("epsb", [128, 1], F32)
    zer = G.sb("zer", [128, 2], F32)
    P.memset('dve', ones[:], 1.0)
    P.memset('dve', epsb[:], EPS)
    P.memset('dve', zer[:], 0.0)
    mc = mod_consts(P, G, modv, n1d, n2d)
    hm = G.sb("h_hm", [128, 2], F32)
    cw = G.sb("h_cw", [128, 3, 24], F32)
    cb = G.sb("h_cb", [128, 24], F32)
    P.dma('sp', hm[:], hmd)
    P.dma('sp', cw[:], cwd)
    P.dma('sp', cb[:], cbd)
    w = G.sb("h_w", [128, 8, 3 * D], BF16)
    load_w_bf16(P, w, wind, 8)
    bk = Banks(G, 8, "hpb")
    xb = [G.sb("h_x%d" % i, [128, 8, NH], F32) for i in range(2)]
    h = G.sb("h_h", [128, 8, NH], BF16)
    tmp = {"sq": G.sb("h_sq", [128, 8, NH], F32), "rs": G.sb("h_rs", [128, NH], F32)}
    pcs = [G.sb("h_pc%d" % i, [128, NH], F32) for i in range(3)]
    p3 = G.sb("h_p3", [128, 24, NT], F32)
    ut = [G.sb("h_ut%d" % i, [128, 8, NT], F32) for i in range(2)]
    xin_v = xin.rearrange("(kc p) t -> p kc t", p=128)
    zin_v = zin.rearrange("(kc p) t -> p kc t", p=128)
    tiles = [("x", i, 0) for i in range(TC // NT)] + [("z", i, 1) for i in range(CTX // NT)]
    for ti, (kind, i, j) in enumerate(tiles):
        x = xb[ti % 2]
        t0 = i * NT
        ntl = (TC if kind == "x" else CTX) // NT
        P.dma('sp', x[:], (xin_v if kind == "x" else zin_v)[:, :, t0:t0 + NH])
        rms_mod(P, tmp, bk, x, h, mc["gm1"], mc["sh1"], j, NH, ones, epsb, "h")
        msk = hm if kind == "x" else zer
        for m in range(24):
            pp = bk.next()
            for kc in range(8):
                P.mm(pp[:, :NH], w[:, kc, m * 128:(m + 1) * 128], h[:, kc, :], start=(kc == 0), stop=(kc == 7))
            pc = pcs[m % 3]
            P.cp('act', pc[:], pp[:, :NH])
            if i == 0:
                P.ts('dve', pc[:, 0:1], pc[:, 0:1], msk[:, 0:1], None, ALU.mult)
            if i == ntl - 1:
                P.ts('dve', pc[:, NH - 1:NH], pc[:, NH - 1:NH], msk[:, 1:2], None, ALU.mult)
            o = p3[:, m, :]
            P.act(o, pc[:, 0:NT], AF.Identity, bias=cb[:, m:m + 1], scale=cw[:, 0, m:m + 1])
            P.stt(o, pc[:, 1:NT + 1], cw[:, 1, m:m + 1], o, ALU.mult, ALU.add)
            P.stt(o, pc[:, 2:NT + 2], cw[:, 2, m:m + 1], o, ALU.mult, ALU.add)
        u = ut[ti % 2]
        P.tt('pool', u[:], p3[:, 8:16, :], p3[:, 16:24, :], ALU.mult)
        ud = (uT if kind == "x" else uTc).rearrange("(kc p) t -> p kc t", p=128)[:, :, t0:t0 + NT]
        xd = (x0T if kind == "x" else x0Tc).rearrange("(kc p) t -> p kc t", p=128)[:, :, t0:t0 + NT]
        P.dma('sp', ud, u[:])
        P.dma('sp', xd, p3[:, 0:8, :])
    G.close()
    if own:
        P.finish()
    return P


def run_l2a(inputs, modv, x_full_T, zT):
    P = build_l2a()
    cwv = np.asarray(inputs["hy_conv_w"][0], np.float32)
    cw = np.ascontiguousarray(cwv.reshape(3, 24, 128).transpose(2, 0, 1))
    cb = np.ascontiguousarray(np.asarray(inputs["hy_conv_b"][0], np.float32).reshape(24, 128).T)
    xpad = np.concatenate([np.zeros((D, 1), np.float32), x_full_T, np.zeros((D, 1), np.float32)], axis=1)
    zh = np.ascontiguousarray(np.concatenate([np.zeros((D, 1), np.float32), zT, np.zeros((D, 1), np.float32)], axis=1))
    common = {"zh": zh, "modv": np.ascontiguousarray(modv[2]), "n1": lay_vec8(inputs["norm1_g"][2]),
              "n2": lay_vec8(inputs["norm2_g"][2]), "hy_w_in": np.ascontiguousarray(inputs["hy_w_in"][0], dtype=np.float32),
              "cw": cw, "cb": cb}
    maps = []
    for c in range(NCORES):
        hmask = np.ones((128, 2), np.float32)
        if c == 0:
            hmask[:, 0] = 0.0
        if c == NCORES - 1:
            hmask[:, 1] = 0.0
        maps.append(dict(common, xh=np.ascontiguousarray(xpad[:, c * TC:c * TC + TC + 2]), hmask=hmask))
    res = run_bass_kernel_spmd(P.nc, maps, core_ids=list(range(NCORES)))
    return res.results


def fft_tables():
    n = NFFT
    a = np.arange(128, dtype=np.float64)
    f2 = np.arange(256, dtype=np.float64)
    b = np.arange(128, dtype=np.float64)
    f1 = np.arange(128, dtype=np.float64)
    th = 2 * np.pi * np.outer(a, f2) / 256.0
    FA = np.concatenate([np.cos(th), -np.sin(th)], axis=1)
    ph = 2 * np.pi * np.outer(b, f2) / n
    TwC, TwS = np.cos(ph), np.sin(ph)
    om = 2 * np.pi * np.outer(b, f1) / 128.0
    C128, S128 = np.cos(om), np.sin(om)
    CS = np.concatenate([C128, S128], axis=1)
    nSC = np.concatenate([-S128, C128], axis=1)
    ph2 = ph.T
    Tw2C = np.stack([np.cos(ph2[c * 128:(c + 1) * 128]) for c in range(2)], axis=1)
    Tw2S = np.stack([np.sin(ph2[c * 128:(c + 1) * 128]) for c in range(2)], axis=1)
    ps = 2 * np.pi * np.outer(f2, a) / 256.0
    CA = np.stack([np.cos(ps[c * 128:(c + 1) * 128]) / n for c in range(2)], axis=1)
    nSA = np.stack([-np.sin(ps[c * 128:(c + 1) * 128]) / n for c in range(2)], axis=1)
    f = lambda x: np.ascontiguousarray(x, dtype=np.float32)
    return dict(FA=f(FA), TwC=f(TwC), TwS=f(TwS), nTwS=f(-TwS), C128=f(C128), S128=f(S128), nS128=f(-S128), CS=f(CS), nSC=f(nSC),
                Tw2C=f(Tw2C), Tw2S=f(Tw2S), nTw2S=f(-Tw2S), CA=f(CA), nSA=f(nSA))


FFT_SHAPES = dict(FA=[128, 512], TwC=[128, 256], TwS=[128, 256], nTwS=[128, 256], C128=[128, 128], S128=[128, 128],
                  nS128=[128, 128], CS=[128, 256], nSC=[128, 256], Tw2C=[128, 2, 128], Tw2S=[128, 2, 128],
                  nTw2S=[128, 2, 128], CA=[128, 2, 128], nSA=[128, 2, 128])
NSLOT = 256
GS = 8


def build_l2c(P=None, filt=None):
    own = P is None
    P = P if P is not None else Prog()
    tabs_d = {k: P.dram_in("T_" + k, shp) for k, shp in FFT_SHAPES.items()}
    Ud = P.dram_in("U", [128, NSLOT, 128])
    if filt is None:
        KFd = P.dram_in("KF", [128, NSLOT, 128])
        KBd = P.dram_in("KB", [128, NSLOT, 128])
    skd = P.dram_in("skipb", [1, NSLOT])
    Yd = P.dram_out("Y", [128, NSLOT, 128])
    G = Scope(P)
    T = {}
    for k, shp in FFT_SHAPES.items():
        T[k] = G.sb("c_" + k, shp, F32)
        P.dma('sp', T[k][:], tabs_d[k])
    skb = G.sb("c_skb", [128, NSLOT], F32)
    P.dma('sp', skb[:], skd.broadcast_to([128, NSLOT]))
    bk = Banks(G, 8, "cpb")
    ub = [G.sb("c_u%d" % i, [128, GS, 128], F32) for i in range(2)]
    kfb = [G.sb("c_kf%d" % i, [128, GS, 128], F32) for i in range(2)]
    kbb = [G.sb("c_kb%d" % i, [128, GS, 128], F32) for i in range(2)]
    yb = [G.sb("c_y%d" % i, [128, GS, 128], F32) for i in range(2)]
    cnt = [0]

    def rot(name, shape, n=2):
        lst = [G.sb("c_%s%d" % (name, i), shape, F32) for i in range(n)]
        return lst

    A_sb = rot("A", [128, 512])
    P1 = rot("P1", [128, 512])
    P2 = rot("P2", [128, 512])
    R1 = rot("R1", [128, 512])
    Fs = rot("Fs", [128, 512], 1)
    Bs = rot("Bs", [128, 512], 1)
    Kt = rot("Kt", [128, 512], 1)
    nKi = rot("nKi", [128, 256], 1)
    Xs = rot("Xs", [128, 512], 1)
    Zs = rot("Zs", [128, 512], 1)
    Gs = rot("Gs", [128, 256])
    Q1 = rot("Q1", [128, 256])
    Q2 = rot("Q2", [128, 256])
    Rc = rot("Rc", [128, 2, 256], 1)

    def cmul(dst, src, c, s, ns, W, fwd, i):
        p1 = (P1 if W == 256 else Q1)[i % 2]
        p2 = (P2 if W == 256 else Q2)[i % 2]
        P.tt('dve', p1[:, :2 * W].rearrange("p (h w) -> p h w", h=2), src.rearrange("p (h w) -> p h w", h=2),
             c.unsqueeze(1).broadcast_to([128, 2, W]), ALU.mult)
        P.tt('pool', p2[:, 0:W], src[:, W:2 * W], s if fwd else ns, ALU.mult)
        P.tt('pool', p2[:, W:2 * W], src[:, 0:W], ns if fwd else s, ALU.mult)
        P.tt('dve', dst, p1[:, :2 * W], p2[:, :2 * W], ALU.add)

    def fft_fwd(x_ap, i):
        ba = bk.next()
        P.mm(ba[:, :], x_ap, T["FA"][:, :], start=True, stop=True)
        asb = A_sb[i % 2]
        P.cp('act', asb[:], ba[:, :])
        r1 = R1[i % 2]
        cmul(r1[:], asb[:], T["TwC"][:], T["TwS"][:], T["nTwS"][:], 256, True, i)
        bx = bk.next()
        P.mm(bx[:, 0:256], T["C128"][:], r1[:, 0:256], start=True, stop=False)
        P.mm(bx[:, 0:256], T["S128"][:], r1[:, 256:512], start=False, stop=True)
        P.mm(bx[:, 256:512], T["C128"][:], r1[:, 256:512], start=True, stop=False)
        P.mm(bx[:, 256:512], T["nS128"][:], r1[:, 0:256], start=False, stop=True)
        return bx

    it = 0
    for g0 in range(0, NSLOT, GS):
        gi = (g0 // GS) % 2
        u, kf, kb, y = ub[gi], kfb[gi], kbb[gi], yb[gi]
        P.dma('sp', u[:], Ud[:, g0:g0 + GS, :])
        if filt is None:
            P.dma('act', kf[:], KFd[:, g0:g0 + GS, :])
            P.dma('sp', kb[:], KBd[:, g0:g0 + GS, :])
        elif g0 < 128:
            P.dma('act', kf[:], filt["kf_l"][g0:g0 + GS, :].rearrange("s (a b) -> a s b", b=128))
            P.dma('sp', kb[:], filt["kb_l"][g0:g0 + GS, :].rearrange("s (a b) -> a s b", b=128))
        else:
            s0 = g0 - 128
            P.memset('pool', kf[:], 0.0)
            P.memset('pool', kb[:], 0.0)
            P.dma('act', kf[0:2, :, :], filt["kf_c"][s0:s0 + GS, :].rearrange("s (a b) -> a s b", b=128))
            P.dma('sp', kb[0:2, :, :], filt["kb_c"][s0:s0 + GS, :].rearrange("s (a b) -> a s b", b=128))
        for sl in range(GS):
            s = g0 + sl
            bF = fft_fwd(kf[:, sl, :], it)
            P.cp('act', Fs[0][:], bF[:, :])
            bB = fft_fwd(kb[:, sl, :], it + 1)
            P.cp('act', Bs[0][:], bB[:, :])
            K = Kt[0]
            P.tt('pool', K[:, 0:256], Fs[0][:, 0:256], Bs[0][:, 0:256], ALU.add)
            P.tt('pool', K[:, 256:512], Fs[0][:, 256:512], Bs[0][:, 256:512], ALU.subtract)
            P.tt('pool', nKi[0][:], Bs[0][:, 256:512], Fs[0][:, 256:512], ALU.subtract)
            P.ts('dve', K[:, 0:256], K[:, 0:256], skb[:, s:s + 1], None, ALU.add)
            bU = fft_fwd(u[:, sl, :], it + 2)
            P.cp('act', Xs[0][:], bU[:, :])
            Z = Zs[0]
            cmul(Z[:], Xs[0][:], K[:, 0:256], K[:, 256:512], nKi[0][:], 256, False, it)
            rc = Rc[0]
            for c in range(2):
                bg = bk.next()
                P.mm(bg[:, 0:256], Z[:, c * 128:(c + 1) * 128], T["CS"][:], start=True, stop=False)
                P.mm(bg[:, 0:256], Z[:, 256 + c * 128:256 + (c + 1) * 128], T["nSC"][:], start=False, stop=True)
                gs = Gs[c]
                P.cp('act', gs[:], bg[:, 0:256])
                cmul(rc[:, c, :], gs[:], T["Tw2C"][:, c, :], T["Tw2S"][:, c, :], T["nTw2S"][:, c, :], 128, False, c)
            by = bk.next()
            for c in range(2):
                P.mm(by[:, 0:128], T["CA"][:, c, :], rc[:, c, 0:128], start=(c == 0), stop=False)
                P.mm(by[:, 0:128], T["nSA"][:, c, :], rc[:, c, 128:256], start=False, stop=(c == 1))
            P.cp('act', y[:, sl, :], by[:, 0:128])
            it += 3
        P.dma('sp', Yd[:, g0:g0 + GS, :], y[:])
    G.close()
    if own:
        P.finish()
    return P


def seq_to_abl(x):
    return np.ascontiguousarray(x.reshape(x.shape[0], 128, 128).transpose(1, 0, 2))


def run_l2c(inputs, f_res, uT_full, uTc):
    P = build_l2c()
    tabs = {"T_" + k: v for k, v in fft_tables().items()}
    skip = np.asarray(inputs["hy_skip"][0], np.float32)
    maps = []
    for c in range(NCORES):
        sl = slice(c * 128, (c + 1) * 128)

        def pad(x):
            o = np.zeros((128, L), np.float32)
            o[:, :x.shape[1]] = x
            return o
        U = np.concatenate([uT_full[sl], pad(uTc[sl])], axis=0)
        KF = np.concatenate([f_res[c]["kf_l"], pad(f_res[c]["kf_c"])], axis=0)
        KB = np.concatenate([f_res[c]["kb_l"], pad(f_res[c]["kb_c"])], axis=0)
        skb = np.ascontiguousarray(np.concatenate([skip[sl], skip[sl]]).reshape(1, NSLOT))
        maps.append(dict(tabs, U=seq_to_abl(U), KF=seq_to_abl(KF), KB=seq_to_abl(KB), skipb=skb))
    res = run_bass_kernel_spmd(P.nc, maps, core_ids=list(range(NCORES)))
    Y = [r["Y"].transpose(1, 0, 2).reshape(NSLOT, L) for r in res.results]
    yT_full = np.ascontiguousarray(np.concatenate([y[:128] for y in Y], axis=0))
    yTc = np.ascontiguousarray(np.concatenate([y[128:, :CTX] for y in Y], axis=0))
    return yT_full, yTc


def build_gate_out(name_w, KC_IN, NT=256, P=None):
    own = P is None
    P = P if P is not None else Prog()
    aT = P.dram_in("aT", [KC_IN * 128, TC])
    bT = P.dram_in("bT", [KC_IN * 128, TC])
    aTc = P.dram_in("aTc", [KC_IN * 128, CTX])
    bTc = P.dram_in("bTc", [KC_IN * 128, CTX])
    xin = P.dram_in("xT", [D, TC])
    zin = P.dram_in("zT", [D, CTX])
    modv = P.dram_in("modv", [128, 48, 2])
    n1d = P.dram_in("n1", [128, 8])
    n2d = P.dram_in("n2", [128, 8])
    wod = P.dram_in(name_w, [KC_IN * 128, D])
    fwin = P.dram_in("ffn_w_in", [D, 2 * FH])
    fwout = P.dram_in("ffn_w_out", [FH, D])
    xout = P.dram_out("xT_out", [D, TC])
    zout = P.dram_out("zT_out", [D, CTX])
    G = Scope(P)
    ones = G.sb("ones", [128, 128], F32)
    epsb = G.sb("epsb", [128, 1], F32)
    P.memset('dve', ones[:], 1.0)
    P.memset('dve', epsb[:], EPS)
    mc = mod_consts(P, G, modv, n1d, n2d)
    v8 = lambda ap: ap.rearrange("(kc p) t -> p kc t", p=128)
    xout_v, zout_v = v8(xout), v8(zout)
    tiles = [(("x", i), NT, 0) for i in range(TC // NT)] + [(("z", i), NT, 1) for i in range(CTX // NT)]

    def dst_out(tn):
        v = xout_v if tn[0] == "x" else zout_v
        return v[:, :, tn[1] * NT:(tn[1] + 1) * NT], ("out",) + tn

    S = Scope(P)
    bk = Banks(S, 8, "opb")
    wo = S.sb("o_wo", [128, KC_IN, D], BF16)
    load_w_bf16(P, wo, wod, KC_IN)
    xb = [S.sb("o_x%d" % i, [128, 8, NT], F32) for i in range(2)]
    ab = [S.sb("o_a%d" % i, [128, KC_IN, NT], F32) for i in range(2)]
    bb = [S.sb("o_b%d" % i, [128, KC_IN, NT], F32) for i in range(2)]
    gb = [S.sb("o_g%d" % i, [128, KC_IN, NT], BF16) for i in range(2)]
    for ti, (tn, nt, j) in enumerate(tiles):
        x, a, b, g = xb[ti % 2], ab[ti % 2], bb[ti % 2], gb[ti % 2]
        t0 = tn[1] * NT
        lat = tn[0] == "x"
        P.dma('sp', x[:], v8(xin if lat else zin)[:, :, t0:t0 + NT])
        P.dma('act', a[:], v8(aT if lat else aTc)[:, :, t0:t0 + NT])
        P.dma('sp', b[:], v8(bT if lat else bTc)[:, :, t0:t0 + NT])
        P.tt('pool', g[:], a[:], b[:], ALU.mult)
        for m in range(8):
            po = bk.next()
            for kc in range(KC_IN):
                P.mm(po[:, :nt], wo[:, kc, m * 128:(m + 1) * 128], g[:, kc, :], start=(kc == 0), stop=(kc == KC_IN - 1))
            P.stt(x[:, m, :], po[:, :nt], mc["g1"][:, m, j:j + 1], x[:, m, :], ALU.mult, ALU.add)
        dst, dkey = dst_out(tn)
        P.dma('sp', dst, x[:], w=[dkey])
    S.close()
    ffn_phase(P, tiles, dst_out, dst_out, mc, fwin, fwout, ones, epsb, NT)
    G.close()
    if own:
        P.finish()
    return P


def run_gate_out(P, name_w, w_out, layer, inputs, modv, aT_sh, bT_sh, aTc, bTc, xT_shards, zT):
    common = {
        "aTc": aTc, "bTc": bTc, "zT": zT, "modv": np.ascontiguousarray(modv[layer]),
        "n1": lay_vec8(inputs["norm1_g"][layer]), "n2": lay_vec8(inputs["norm2_g"][layer]),
        name_w: np.ascontiguousarray(w_out, dtype=np.float32),
        "ffn_w_in": np.ascontiguousarray(inputs["ffn_w_in"][layer], dtype=np.float32),
        "ffn_w_out": np.ascontiguousarray(inputs["ffn_w_out"][layer], dtype=np.float32),
    }
    maps = [dict(common, aT=aT_sh[i], bT=bT_sh[i], xT=xT_shards[i]) for i in range(NCORES)]
    res = run_bass_kernel_spmd(P.nc, maps, core_ids=list(range(NCORES)))
    return [r["xT_out"] for r in res.results], res.results[0]["zT_out"]


def shard_T(xT_full):
    return [np.ascontiguousarray(xT_full[:, i * TC:(i + 1) * TC]) for i in range(NCORES)]


SI = 2048
SH = 32
SP_ = 64
SG = 4
SN = 128
SPROJ = 5184


def build_l3a(NT=256, P=None):
    own = P is None
    P = P if P is not None else Prog()
    NH = NT + 2
    xin = P.dram_in("xh", [D, TC + 2])
    zin = P.dram_in("zh", [D, CTX + 2])
    hmd = P.dram_in("hmask", [128, 2])
    modv = P.dram_in("modv", [128, 48, 2])
    n1d = P.dram_in("n1", [128, 8])
    n2d = P.dram_in("n2", [128, 8])
    wind = P.dram_in("ssd_w_in", [D, SPROJ])
    cwd = P.dram_in("cw", [128, 3, 24])
    cbd = P.dram_in("cb", [128, 24])
    dtb_d = P.dram_in("dtb", [1, 64])
    ident_d = P.dram_in("ident", [128, 128])
    outs = {}
    for tag, T_ in (("", TC), ("c", CTX)):
        outs["sz" + tag] = P.dram_out("sz_tok" + tag, [T_, SI])
        outs["xs" + tag] = P.dram_out("xs_tok" + tag, [T_, SI])
        outs["bt" + tag] = P.dram_out("B_tok" + tag, [T_, 512])
        outs["BT" + tag] = P.dram_out("BT" + tag, [512, T_])
        outs["CT" + tag] = P.dram_out("CT" + tag, [512, T_])
        outs["dt" + tag] = P.dram_out("dt_tok" + tag, [T_, 64])
    G = Scope(P)
    ones = G.sb("ones", [128, 128], F32)
    epsb = G.sb("epsb", [128, 1], F32)
    zer = G.sb("zer", [128, 2], F32)
    P.memset('dve', ones[:], 1.0)
    P.memset('dve', epsb[:], EPS)
    P.memset('dve', zer[:], 0.0)
    mc = mod_consts(P, G, modv, n1d, n2d)
    hm = G.sb("h_hm", [128, 2], F32)
    cw = G.sb("h_cw", [128, 3, 24], F32)
    cb = G.sb("h_cb", [128, 24], F32)
    dtb = G.sb("h_dtb", [128, 64], F32)
    ident = G.sb("h_ident", [128, 128], F32)
    P.dma('sp', hm[:], hmd)
    P.dma('sp', cw[:], cwd)
    P.dma('sp', cb[:], cbd)
    P.dma('sp', dtb[:], dtb_d.broadcast_to([128, 64]))
    P.dma('sp', ident[:], ident_d)
    w = G.sb("h_w", [128, 8, SPROJ], BF16)
    load_w_bf16(P, w, wind, 8)
    bk = Banks(G, 8, "hpb")
    xb = [G.sb("h_x%d" % i, [128, 8, NH], F32) for i in range(2)]
    h = G.sb("h_h", [128, 8, NH], BF16)
    tmp = {"sq": G.sb("h_sq", [128, 8, NH], F32), "rs": G.sb("h_rs", [128, NH], F32)}
    pcs = [G.sb("h_pc%d" % i, [128, NH], F32) for i in range(3)]
    p3 = G.sb("h_p3", [128, 24, NT], F32)
    tok = [G.sb("h_tok%d" % i, [128, SI], F32) for i in range(2)]
    btk = [G.sb("h_btk%d" % i, [128, 512], F32) for i in range(2)]
    szt = [G.sb("h_sz%d" % i, [128, SI], F32) for i in range(2)]
    dtt = [G.sb("h_dt%d" % i, [128, 64], F32) for i in range(2)]
    xin_v = xin.rearrange("(kc p) t -> p kc t", p=128)
    zin_v = zin.rearrange("(kc p) t -> p kc t", p=128)
    tiles = [("x", i, 0) for i in range(TC // NT)] + [("z", i, 1) for i in range(CTX // NT)]
    XB0 = SI
    DT0 = SI + 3072
    cc = 0
    for ti, (kind, i, j) in enumerate(tiles):
        x = xb[ti % 2]
        t0 = i * NT
        tag = "" if kind == "x" else "c"
        ntl = (TC if kind == "x" else CTX) // NT
        P.dma('sp', x[:], (xin_v if kind == "x" else zin_v)[:, :, t0:t0 + NH])
        rms_mod(P, tmp, bk, x, h, mc["gm1"], mc["sh1"], j, NH, ones, epsb, "h")
        msk = hm if kind == "x" else zer
        for m in range(24):
            pp = bk.next()
            for kc in range(8):
                P.mm(pp[:, :NH], w[:, kc, XB0 + m * 128:XB0 + (m + 1) * 128], h[:, kc, :], start=(kc == 0), stop=(kc == 7))
            pc = pcs[m % 3]
            P.cp('act', pc[:], pp[:, :NH])
            if i == 0:
                P.ts('dve', pc[:, 0:1], pc[:, 0:1], msk[:, 0:1], None, ALU.mult)
            if i == ntl - 1:
                P.ts('dve', pc[:, NH - 1:NH], pc[:, NH - 1:NH], msk[:, 1:2], None, ALU.mult)
            o = p3[:, m, :]
            P.act(o, pc[:, 0:NT], AF.Identity, bias=cb[:, m:m + 1], scale=cw[:, 0, m:m + 1])
            P.stt(o, pc[:, 1:NT + 1], cw[:, 1, m:m + 1], o, ALU.mult, ALU.add)
            P.stt(o, pc[:, 2:NT + 2], cw[:, 2, m:m + 1], o, ALU.mult, ALU.add)
            P.act(o, o, AF.Silu)
        P.dma('sp', outs["BT" + tag].rearrange("(g n) t -> n g t", n=128)[:, :, t0:t0 + NT], p3[:, 16:20, :])
        P.dma('sp', outs["CT" + tag].rearrange("(g n) t -> n g t", n=128)[:, :, t0:t0 + NT], p3[:, 20:24, :])
        for c2 in range(NT // 128):
            ts0 = t0 + c2 * 128
            tk, bt_, sz, dt_ = tok[cc % 2], btk[cc % 2], szt[cc % 2], dtt[cc % 2]
            cc += 1
            cs = slice(c2 * 128, (c2 + 1) * 128)
            hs = slice(1 + c2 * 128, 1 + (c2 + 1) * 128)
            for q in range(5):
                pt = bk.next()
                for e in range(4):
                    P.tr(pt[:, e * 128:(e + 1) * 128], p3[:, q * 4 + e, cs], ident[:])
                dst = tk[:, q * 512:(q + 1) * 512] if q < 4 else bt_[:, :]
                P.cp('act' if q % 2 == 0 else 'dve', dst, pt[:, :])
            P.dma('sp', outs["xs" + tag][ts0:ts0 + 128, :], tk[:])
            P.dma('sp', outs["bt" + tag][ts0:ts0 + 128, :], bt_[:])
            for q in range(4):
                pz = bk.next()
                for kc in range(8):
                    P.mm(pz[:, :], h[:, kc, hs], w[:, kc, q * 512:(q + 1) * 512], start=(kc == 0), stop=(kc == 7))
                P.act(sz[:, q * 512:(q + 1) * 512], pz[:, :], AF.Silu)
            P.dma('sp', outs["sz" + tag][ts0:ts0 + 128, :], sz[:])
            pd = bk.next()
            for kc in range(8):
                P.mm(pd[:, 0:64], h[:, kc, hs], w[:, kc, DT0:DT0 + 64], start=(kc == 0), stop=(kc == 7))
            P.tt('dve', dt_[:], pd[:, 0:64], dtb[:], ALU.add)
            P.act(dt_[:], dt_[:], AF.Exp)
            P.act(dt_[:], dt_[:], AF.Ln, bias=ones[:, 0:1], scale=1.0)
            P.dma('sp', outs["dt" + tag][ts0:ts0 + 128, :], dt_[:])
    G.close()
    if own:
        P.finish()
    return P


def run_l3a(inputs, modv, x_full_T, zT):
    P = build_l3a()
    cwv = np.asarray(inputs["ssd_conv_w"][0], np.float32)
    cw = np.ascontiguousarray(cwv.reshape(3, 24, 128).transpose(2, 0, 1))
    cb = np.ascontiguousarray(np.asarray(inputs["ssd_conv_b"][0], np.float32).reshape(24, 128).T)
    xpad = np.concatenate([np.zeros((D, 1), np.float32), x_full_T, np.zeros((D, 1), np.float32)], axis=1)
    zh = np.ascontiguousarray(np.concatenate([np.zeros((D, 1), np.float32), zT, np.zeros((D, 1), np.float32)], axis=1))
    common = {"zh": zh, "modv": np.ascontiguousarray(modv[3]), "n1": lay_vec8(inputs["norm1_g"][3]),
              "n2": lay_vec8(inputs["norm2_g"][3]), "ssd_w_in": np.ascontiguousarray(inputs["ssd_w_in"][0], dtype=np.float32),
              "cw": cw, "cb": cb, "dtb": np.ascontiguousarray(np.asarray(inputs["ssd_dt_bias"][0], np.float32).reshape(1, 64)),
              "ident": np.eye(128, dtype=np.float32)}
    maps = []
    for c in range(NCORES):
        hmask = np.ones((128, 2), np.float32)
        if c == 0:
            hmask[:, 0] = 0.0
        if c == NCORES - 1:
            hmask[:, 1] = 0.0
        maps.append(dict(common, xh=np.ascontiguousarray(xpad[:, c * TC:c * TC + TC + 2]), hmask=hmask))
    res = run_bass_kernel_spmd(P.nc, maps, core_ids=list(range(NCORES)))
    return res.results


def tri_consts():
    k = np.arange(128)
    tf = (k[:, None] <= k[None, :]).astype(np.float32)
    tb = (k[:, None] >= k[None, :]).astype(np.float32)
    return tf, tb


def build_l3b(need_y, P=None):
    own = P is None
    P = P if P is not None else Prog()
    xs_d = P.dram_in("xs_tok", [TC, SI])
    bt_d = P.dram_in("B_tok", [TC, 512])
    BT_d = P.dram_in("BT", [512, TC])
    CT_d = P.dram_in("CT", [512, TC])
    dt_d = P.dram_in("dt_tok", [TC, 64])
    xsc_d = P.dram_in("xs_tokc", [CTX, SI])
    btc_d = P.dram_in("B_tokc", [CTX, 512])
    dtc_d = P.dram_in("dt_tokc", [CTX, 64])
    alog_d = P.dram_in("alog", [1, 64])
    trif_d = P.dram_in("tri_f", [128, 128])
    trib_d = P.dram_in("tri_b", [128, 128])
    if need_y:
        Hs_d = P.dram_in("Hs", [NCORES, 2, SG, 128, 512])
        Ds_d = P.dram_in("Ds", [1, NCORES * 2 * SH])
        fm_d = P.dram_in("fm", [1, 2 * NCORES])
        y_d = [P.dram_out("yf_tok", [TC, SI]), P.dram_out("yb_tok", [TC, SI])]
    else:
        Hloc_d = P.dram_out("Hloc", [2, SG, 128, 512])
        Dtot_d = P.dram_out("Dtot", [2, 128, SH])
    G = Scope(P)
    ones = G.sb("ones", [128, 128], F32)
    P.memset('dve', ones[:], 1.0)
    tri = [G.sb("tri_f_s", [128, 128], F32), G.sb("tri_b_s", [128, 128], F32)]
    P.dma('sp', tri[0][:], trif_d)
    P.dma('sp', tri[1][:], trib_d)
    Abc = G.sb("Abc", [128, 64], F32)
    P.dma('sp', Abc[:], alog_d.broadcast_to([128, 64]))
    P.act(Abc[:], Abc[:], AF.Exp)
    P.ts('dve', Abc[:], Abc[:], -1.0, None, ALU.mult)
    BTa = G.sb("BTa", [128, SG, TC], BF16)
    CTa = G.sb("CTa", [128, SG, TC], BF16)
    P.dma('pool', BTa[:], BT_d.rearrange("(g n) t -> n g t", n=128))
    P.dma('pool', CTa[:], CT_d.rearrange("(g n) t -> n g t", n=128))
    hT = [[G.sb("hT%d%d" % (d, g), [128, 512], F32) for g in range(SG)] for d in range(2)]
    hTb = [[G.sb("hTb%d%d" % (d, g), [128, 512], BF16) for g in range(SG)] for d in range(2)]
    lsum = [G.sb("lsum%d" % d, [128, SH], F32) for d in range(2)]
    for d in range(2):
        P.memset('dve', lsum[d][:], 0.0)
        for g in range(SG):
            P.memset('pool', hT[d][g][:], 0.0)
            P.memset('pool', hTb[d][g][:], 0.0)
    bk = Banks(G, 8, "bpb")
    xbuf = [G.sb("xbuf%d" % i, [128, SI], BF16) for i in range(2)]
    bbuf = [G.sb("bbuf%d" % i, [128, 512], BF16) for i in range(2)]
    dbuf = [G.sb("dbuf%d" % i, [128, 64], F32) for i in range(2)]
    a_t = G.sb("a_t", [128, SH], F32)
    acs = G.sb("acs", [128, SH], F32)
    last = G.sb("last", [128, SH], F32)
    w_t = G.sb("w_t", [128, SH], F32)
    dec = G.sb("dec", [128, SH], F32)
    eacs = G.sb("eacs", [128, SH], F32)
    rhsb = G.sb("rhsb", [128, 8, 128], F32)
    cbm = G.sb("cbm", [128, 128], F32)
    seg = [G.sb("seg%d" % i, [128, 128], F32) for i in range(2)]
    Et = [G.sb("Et%d" % i, [128, 128], F32) for i in range(2)]
    MT = [G.sb("MT%d" % i, [128, 128], BF16) for i in range(2)]
    xw = [G.sb("xw%d" % i, [128, 512], BF16) for i in range(2)]
    ytmp = G.sb("ytmp", [128, 512], F32)
    yout = [G.sb("yout%d" % i, [128, SI], F32) for i in range(2)]
    cnt = [0]

    def chunk(d, xsrc, bsrc, dsrc, tok0, full, BTg, CTg, ydst):
        i = cnt[0]
        cnt[0] += 1
        xb, bb, db = xbuf[i % 2], bbuf[i % 2], dbuf[i % 2]
        P.dma('pool', xb[:], xsrc[tok0:tok0 + 128, :])
        P.dma('pool', bb[:], bsrc[tok0:tok0 + 128, :])
        P.dma('sp', db[:], dsrc[tok0:tok0 + 128, :])
        dts = db[:, d * SH:(d + 1) * SH]
        P.tt('dve', a_t[:], dts, Abc[:, d * SH:(d + 1) * SH], ALU.mult)
        pa = bk.next()
        P.mm(pa[:, 0:SH], tri[d][:], a_t[:], start=True, stop=True)
        P.cp('act', acs[:], pa[:, 0:SH])
        pl = bk.next()
        P.mm(pl[:, 0:SH], ones[:], a_t[:], start=True, stop=True)
        P.cp('act', last[:], pl[:, 0:SH])
        P.tt('dve', w_t[:], last[:], acs[:], ALU.subtract)
        P.act(w_t[:], w_t[:], AF.Exp)
        P.tt('dve', w_t[:], w_t[:], dts, ALU.mult)
        P.act(dec[:], last[:], AF.Exp)
        P.tt('dve', lsum[d][:], lsum[d][:], last[:], ALU.add)
        if full:
            P.act(eacs[:], acs[:], AF.Exp)
            yo = yout[i % 2]
        for g in range(SG):
            hs = slice(g * 8, (g + 1) * 8)
            if full:
                P.tt('pool', rhsb[:], a_t[:, hs].unsqueeze(2).broadcast_to([128, 8, 128]),
                     tri[d][:].unsqueeze(1).broadcast_to([128, 8, 128]), ALU.mult)
                pb = [bk.next(), bk.next()]
                for q in range(2):
                    P.mm(pb[q][:, :], ones[:], rhsb[:, q * 4:(q + 1) * 4, :].rearrange("p a b -> p (a b)"), start=True, stop=True)
                pcb = bk.next()
                P.mm(pcb[:, 0:128], BTg(g), CTg(g), start=True, stop=True)
                P.tt('dve', cbm[:], pcb[:, 0:128], tri[d][:], ALU.mult)
                pyd = bk.next()
                for r in range(8):
                    h = g * 8 + r
                    pbr = pb[r // 4][:, (r % 4) * 128:(r % 4 + 1) * 128]
                    sg_, e_, m_ = seg[r % 2], Et[r % 2], MT[r % 2]
                    P.ts('dve', sg_[:], pbr, acs[:, h:h + 1], 0.0, ALU.subtract, ALU.min)
                    P.act(e_[:], sg_[:], AF.Exp)
                    P.stt(m_[:], e_[:], db[:, d * SH + h:d * SH + h + 1], cbm[:], ALU.mult, ALU.mult)
                    P.mm(pyd[:, r * 64:(r + 1) * 64], m_[:], xb[:, h * 64:(h + 1) * 64], start=True, stop=True)
                pyo = bk.next()
                P.mm(pyo[:, :], CTg(g), hTb[d][g][:], start=True, stop=True)
                P.tt('dve', ytmp[:].rearrange("p (r c) -> p r c", c=64), pyo[:, :].rearrange("p (r c) -> p r c", c=64),
                     eacs[:, hs].unsqueeze(2).broadcast_to([128, 8, 64]), ALU.mult)
                P.tt('dve', yo[:, g * 512:(g + 1) * 512], ytmp[:], pyd[:, :], ALU.add)
            xw_ = xw[g % 2]
            P.tt('pool', xw_[:].rearrange("p (r c) -> p r c", c=64), xb[:, g * 512:(g + 1) * 512].rearrange("p (r c) -> p r c", c=64),
                 w_t[:, hs].unsqueeze(2).broadcast_to([128, 8, 64]), ALU.mult)
            pst = bk.next()
            P.mm(pst[:, :], bb[:, g * 128:(g + 1) * 128], xw_[:], start=True, stop=True)
            hv = hT[d][g][:].rearrange("p (r c) -> p r c", c=64)
            P.tt('dve', hv, hv, dec[:, hs].unsqueeze(2).broadcast_to([128, 8, 64]), ALU.mult)
            P.tt('dve', hT[d][g][:], hT[d][g][:], pst[:, :], ALU.add)
            P.cp('pool', hTb[d][g][:], hT[d][g][:])
        if full:
            P.dma('sp', ydst[tok0:tok0 + 128, :], yo[:])

    nch = TC // 128
    if need_y:
        Dsb = G.sb("Dsb", [128, NCORES * 2 * SH], F32)
        fmb = G.sb("fmb", [128, 2 * NCORES], F32)
        P.dma('sp', Dsb[:], Ds_d.broadcast_to([128, NCORES * 2 * SH]))
        P.dma('sp', fmb[:], fm_d.broadcast_to([128, 2 * NCORES]))
        Deff = G.sb("Deff", [128, SH], F32)
        Hk = [G.sb("Hk%d" % i, [128, 512], F32) for i in range(2)]
        hi = 0
        for d in range(2):
            for cc in ((0, 1) if d == 0 else (1, 0)):
                chunk(d, xsc_d, btc_d, dtc_d, cc * 128, False, None, None, None)
            for k in (range(NCORES) if d == 0 else range(NCORES - 1, -1, -1)):
                m = fmb[:, d * NCORES + k:d * NCORES + k + 1]
                o = (k * 2 + d) * SH
                P.ts('dve', Deff[:], Dsb[:, o:o + SH], -1.0, m, ALU.add, ALU.mult)
                P.ts('dve', Deff[:], Deff[:], 1.0, None, ALU.add)
                for g in range(SG):
                    hk = Hk[hi % 2]
                    hi += 1
                    P.dma('sp', hk[:], Hs_d[k, d, g])
                    hv = hT[d][g][:].rearrange("p (r c) -> p r c", c=64)
                    P.tt('dve', hv, hv, Deff[:, g * 8:(g + 1) * 8].unsqueeze(2).broadcast_to([128, 8, 64]), ALU.mult)
                    P.stt(hT[d][g][:], hk[:], m, hT[d][g][:], ALU.mult, ALU.add)
            for g in range(SG):
                P.cp('pool', hTb[d][g][:], hT[d][g][:])
    for d in range(2):
        for c in (range(nch) if d == 0 else range(nch - 1, -1, -1)):
            t0 = c * 128
            chunk(d, xs_d, bt_d, dt_d, t0, need_y, lambda g, t0=t0: BTa[:, g, t0:t0 + 128], lambda g, t0=t0: CTa[:, g, t0:t0 + 128],
                  y_d[d] if need_y else None)
    if not need_y:
        for d in range(2):
            P.act(lsum[d][:], lsum[d][:], AF.Exp)
            P.dma('sp', Dtot_d[d], lsum[d][:])
            for g in range(SG):
                P.dma('sp', Hloc_d[d, g], hT[d][g][:])
    G.close()
    if own:
        P.finish()
    return P


def run_l3b(inputs, a_res, summaries=None):
    need_y = summaries is not None
    P = build_l3b(need_y)
    tf, tb = tri_consts()
    common = {"xs_tokc": a_res[0]["xs_tokc"], "B_tokc": a_res[0]["B_tokc"], "dt_tokc": a_res[0]["dt_tokc"],
              "alog": np.ascontiguousarray(np.asarray(inputs["ssd_a_log"][0], np.float32).reshape(1, 64)), "tri_f": tf, "tri_b": tb}
    if need_y:
        common["Hs"] = np.ascontiguousarray(np.stack([s["Hloc"] for s in summaries]))
        common["Ds"] = np.ascontiguousarray(np.stack([s["Dtot"][:, 0, :] for s in summaries]).reshape(1, -1))
    maps = []
    for c in range(NCORES):
        m = dict(common, xs_tok=a_res[c]["xs_tok"], B_tok=a_res[c]["B_tok"], BT=a_res[c]["BT"], CT=a_res[c]["CT"], dt_tok=a_res[c]["dt_tok"])
        if need_y:
            fm = np.zeros((1, 2 * NCORES), np.float32)
            fm[0, :c] = 1.0
            fm[0, NCORES + c + 1:] = 1.0
            m["fm"] = fm
        maps.append(m)
    res = run_bass_kernel_spmd(P.nc, maps, core_ids=list(range(NCORES)))
    return res.results


def build_l3c(NT=256, P=None):
    own = P is None
    P = P if P is not None else Prog()
    yf_d = P.dram_in("yf_tok", [TC, SI])
    yb_d = P.dram_in("yb_tok", [TC, SI])
    xs_d = P.dram_in("xs_tok", [TC, SI])
    sz_d = P.dram_in("sz_tok", [TC, SI])
    dsk_d = P.dram_in("dskip", [1, SI])
    ng_d = P.dram_in("normg", [1, SI])
    ident_d = P.dram_in("ident", [128, 128])
    xin = P.dram_in("xT", [D, TC])
    modv = P.dram_in("modv", [128, 48, 2])
    n1d = P.dram_in("n1", [128, 8])
    n2d = P.dram_in("n2", [128, 8])
    wod = P.dram_in("ssd_w_out", [SI, D])
    fwin = P.dram_in("ffn_w_in", [D, 2 * FH])
    fwout = P.dram_in("ffn_w_out", [FH, D])
    fg_d = P.dram_in("final_g", [128, 8])
    xmid = P.dram_tmp("xmid", [D, TC])
    out = P.dram_out("outT", [D, TC])
    G = Scope(P)
    ones = G.sb("ones", [128, 128], F32)
    epsb = G.sb("epsb", [128, 1], F32)
    P.memset('dve', ones[:], 1.0)
    P.memset('dve', epsb[:], EPS)
    mc = mod_consts(P, G, modv, n1d, n2d)
    v8 = lambda ap: ap.rearrange("(kc p) t -> p kc t", p=128)

    S = Scope(P)
    bk = Banks(S, 8, "cpb")
    ident = S.sb("c_ident", [128, 128], F32)
    dsk = S.sb("c_dsk", [128, SI], F32)
    ng = S.sb("c_ng", [128, SI], F32)
    P.dma('sp', ident[:], ident_d)
    P.dma('sp', dsk[:], dsk_d.broadcast_to([128, SI]))
    P.dma('sp', ng[:], ng_d.broadcast_to([128, SI]))
    wo = S.sb("c_wo", [128, 16, D], BF16)
    load_w_bf16(P, wo, wod, 16)
    yA = [S.sb("c_yA%d" % i, [128, SI], F32) for i in range(2)]
    yB = [S.sb("c_yB%d" % i, [128, SI], F32) for i in range(2)]
    xsb = [S.sb("c_xs%d" % i, [128, SI], F32) for i in range(2)]
    szb = [S.sb("c_sz%d" % i, [128, SI], F32) for i in range(2)]
    st = S.sb("c_st", [128, 4, 6], F32)
    mv = S.sb("c_mv", [128, 4, 2], F32)
    ms = S.sb("c_ms", [128, 4], F32)
    gT = S.sb("c_gT", [128, 16, 128], BF16)
    xt = [S.sb("c_x%d" % i, [128, 8, 128], F32) for i in range(2)]
    for c in range(TC // 128):
        t0 = c * 128
        a, b, xs, sz, x = yA[c % 2], yB[c % 2], xsb[c % 2], szb[c % 2], xt[c % 2]
        P.dma('sp', a[:], yf_d[t0:t0 + 128, :])
        P.dma('act', b[:], yb_d[t0:t0 + 128, :])
        P.dma('sp', xs[:], xs_d[t0:t0 + 128, :])
        P.dma('act', sz[:], sz_d[t0:t0 + 128, :])
        P.dma('sp', x[:], v8(xin)[:, :, t0:t0 + 128])
        P.tt('pool', a[:], a[:], b[:], ALU.add)
        P.tt('pool', xs[:], xs[:], dsk[:], ALU.mult)
        P.tt('dve', a[:], a[:], xs[:], ALU.add)
        P.tt('pool', a[:], a[:], sz[:], ALU.mult)
        for g in range(4):
            P.I('dve', 'bn_stats', st[:, g, :], a[:, g * 512:(g + 1) * 512], r=[a], w=[st])
        for g in range(4):
            P.I('dve', 'bn_aggr', mv[:, g, :], st[:, g, :], r=[st], w=[mv])
        P.stt(ms[:], mv[:, :, 0], 1.0, mv[:, :, 0], ALU.mult, ALU.mult)
        P.tt('dve', ms[:], ms[:], mv[:, :, 1], ALU.add)
        P.act(ms[:], ms[:], AF.Sqrt, bias=epsb[:, 0:1], scale=1.0)
        P.recip(ms[:], ms[:])
        P.tt('dve', a[:].rearrange("p (g c) -> p g c", g=4), a[:].rearrange("p (g c) -> p g c", g=4),
             ms[:].unsqueeze(2).broadcast_to([128, 4, 512]), ALU.mult)
        P.tt('pool', a[:], a[:], ng[:], ALU.mult)
        for q in range(4):
            pt = bk.next()
            for e in range(4):
                fc = q * 4 + e
                P.tr(pt[:, e * 128:(e + 1) * 128], a[:, fc * 128:(fc + 1) * 128], ident[:])
            P.cp('act' if q % 2 == 0 else 'dve', gT[:, q * 4:(q + 1) * 4, :], pt[:, :].rearrange("p (a b) -> p a b", b=128))
        for m in range(8):
            po = bk.next()
            for kc in range(16):
                P.mm(po[:, 0:128], wo[:, kc, m * 128:(m + 1) * 128], gT[:, kc, :], start=(kc == 0), stop=(kc == 15))
            P.stt(x[:, m, :], po[:, 0:128], mc["g1"][:, m, 0:1], x[:, m, :], ALU.mult, ALU.add)
        P.dma('sp', v8(xmid)[:, :, t0:t0 + 128], x[:], w=[("mid", c // 2)])
    S.close()

    tiles = [(("x", i), NT, 0) for i in range(TC // NT)]

    def mid_of(tn):
        return v8(xmid)[:, :, tn[1] * NT:(tn[1] + 1) * NT], ("mid", tn[1])
    ffn_phase(P, tiles, mid_of, mid_of, mc, fwin, fwout, ones, epsb, NT)

    S = Scope(P)
    bk = Banks(S, 4, "npb")
    fg = S.sb("n_fg", [128, 8], F32)
    P.dma('sp', fg[:], fg_d)
    xb = [S.sb("n_x%d" % i, [128, 8, NT], F32) for i in range(2)]
    sq = S.sb("n_sq", [128, 8, NT], F32)
    rs = S.sb("n_rs", [128, NT], F32)
    for ti, (tn, nt, j) in enumerate(tiles):
        x = xb[ti % 2]
        src, key = mid_of(tn)
        P.dma('sp', x[:], src, r=[key])
        P.tt('pool', sq[:], x[:], x[:], ALU.mult)
        pt = bk.next()
        for kc in range(8):
            P.mm(pt[:, :NT], ones[:, :], sq[:, kc, :], start=(kc == 0), stop=(kc == 7))
        P.act(rs[:], pt[:, :NT], AF.Sqrt, bias=epsb[:, 0:1], scale=1.0 / D)
        P.recip(rs[:], rs[:])
        P.tt('dve', sq[:], x[:], rs[:].unsqueeze(1).broadcast_to([128, 8, NT]), ALU.mult)
        for kc in range(8):
            P.ts('dve', sq[:, kc, :], sq[:, kc, :], fg[:, kc:kc + 1], None, ALU.mult)
        P.dma('sp', v8(out)[:, :, tn[1] * NT:(tn[1] + 1) * NT], sq[:])
    S.close()
    G.close()
    if own:
        P.finish()
    return P


def run_l3c(inputs, modv, a_res, y_res, xT_shards):
    P = build_l3c()
    common = {
        "dskip": np.ascontiguousarray(np.repeat(np.asarray(inputs["ssd_d_skip"][0], np.float32), 64).reshape(1, SI)),
        "normg": np.ascontiguousarray(np.asarray(inputs["ssd_norm_g"][0], np.float32).reshape(1, SI)),
        "ident": np.eye(128, dtype=np.float32), "modv": np.ascontiguousarray(modv[3]),
        "n1": lay_vec8(inputs["norm1_g"][3]), "n2": lay_vec8(inputs["norm2_g"][3]),
        "ssd_w_out": np.ascontiguousarray(inputs["ssd_w_out"][0], dtype=np.float32),
        "ffn_w_in": np.ascontiguousarray(inputs["ffn_w_in"][3], dtype=np.float32),
        "ffn_w_out": np.ascontiguousarray(inputs["ffn_w_out"][3], dtype=np.float32),
        "final_g": lay_vec8(inputs["final_g"]),
    }
    maps = [dict(common, yf_tok=y_res[i]["yf_tok"], yb_tok=y_res[i]["yb_tok"], xs_tok=a_res[i]["xs_tok"], sz_tok=a_res[i]["sz_tok"],
                 xT=xT_shards[i]) for i in range(NCORES)]
    res = run_bass_kernel_spmd(P.nc, maps, core_ids=list(range(NCORES)))
    return [r["outT"] for r in res.results]


import sys
import time


def _log(msg):
    sys.stderr.write("[kernel] %s\n" % msg)
    sys.stderr.flush()


def _launch(P, maps, tag):
    t0 = time.time()
    ext = set(P.ext_in)
    maps = [{k: v for k, v in m.items() if k in ext} for m in maps]
    missing = ext - set(maps[0].keys())
    assert not missing, (tag, sorted(missing))
    res = run_bass_kernel_spmd(P.nc, maps, core_ids=list(range(len(maps))))
    nb = sum(v.nbytes for m in maps for v in m.values())
    _log("launch %s: %d instr, in %.0f MB, %.1f s" % (tag, P.ninst, nb / 1e6, time.time() - t0))
    return res.results


def _pf(pfx, d):
    return {pfx + k: v for k, v in d.items()}


def _f32(a):
    return np.ascontiguousarray(a, dtype=np.float32)


def _halo_inputs(x_full_T, zT, c):
    xpad = np.concatenate([np.zeros((D, 1), np.float32), x_full_T, np.zeros((D, 1), np.float32)], axis=1)
    zh = np.ascontiguousarray(np.concatenate([np.zeros((D, 1), np.float32), zT, np.zeros((D, 1), np.float32)], axis=1))
    hmask = np.ones((128, 2), np.float32)
    if c == 0:
        hmask[:, 0] = 0.0
    if c == NCORES - 1:
        hmask[:, 1] = 0.0
    return {"xh": np.ascontiguousarray(xpad[:, c * TC:c * TC + TC + 2]), "zh": zh, "hmask": hmask}


def _conv_lay(cw_, cb_):
    cw = np.ascontiguousarray(np.asarray(cw_, np.float32).reshape(3, 24, 128).transpose(2, 0, 1))
    cb = np.ascontiguousarray(np.asarray(cb_, np.float32).reshape(24, 128).T)
    return cw, cb


def _norms(inputs, modv, l):
    return {"modv": np.ascontiguousarray(modv[l]), "n1": lay_vec8(inputs["norm1_g"][l]), "n2": lay_vec8(inputs["norm2_g"][l])}


def _ffn(inputs, l):
    return {"ffn_w_in": _f32(inputs["ffn_w_in"][l]), "ffn_w_out": _f32(inputs["ffn_w_out"][l])}


def launch_A(inputs, modv, xT_shards, zT):
    P = Prog()
    P.pfx = "a0_"
    build_l0(P=P)
    P.bind["a1_xT"] = P.made["a0_xT_out"]
    P.bind["a1_zT"] = P.made["a0_zT_out"]
    P.pfx = "a1_"
    build_l1a(P=P)
    P.finish()
    ws = np.asarray(inputs["gm_ws"], np.float32)[0]
    bs = np.asarray(inputs["gm_bs"], np.float32)[0]
    c0 = dict(_norms(inputs, modv, 0), **_ffn(inputs, 0))
    c0.update({"zT": zT, "gm_w_in": _f32(inputs["gm_w_in"][0]), "gm_w_out": _f32(inputs["gm_w_out"][0]),
               "ln_g": _f32(inputs["gm_ln_g"][0].reshape(1, GW)), "ln_b": _f32(inputs["gm_ln_b"][0].reshape(1, GW)),
               "wsT": np.ascontiguousarray(ws.transpose(2, 0, 1)),
               "bsfc": np.ascontiguousarray(np.repeat(bs, 2, axis=0).reshape(1, 16 * 128))})
    c1 = dict(_norms(inputs, modv, 1))
    c1.update({"w_qkv": _f32(inputs["at_w_qkv"][0]), "qg": _f32(np.asarray(inputs["at_q_g"][0]).reshape(64, 1)),
               "kg": _f32(np.asarray(inputs["at_k_g"][0]).reshape(64, 1)), "rotm": rot_matrix()})
    maps = []
    for i in range(NCORES):
        cs, sn = rope_tables(i * TC, TC)
        m = _pf("a0_", dict(c0, xT=xT_shards[i]))
        m.update(_pf("a1_", dict(c1, cosT=cs, sinT=sn)))
        maps.append(m)
    res = _launch(P, maps, "A")
    x0 = [r["a0_xT_out"] for r in res]
    z0 = res[0]["a0_zT_out"]
    a_res = [{k: r["a1_" + k] for k in ("QT", "KT", "Vtok", "QTc", "KTc", "Vtokc")} for r in res]
    return x0, z0, a_res


def launch_C(inputs, modv, x_full_T, zT):
    P = Prog()
    P.internal = {"x0T", "x0Tc"}
    build_l2a(P=P)
    P.finish()
    cw, cb = _conv_lay(inputs["hy_conv_w"][0], inputs["hy_conv_b"][0])
    common = dict(_norms(inputs, modv, 2), hy_w_in=_f32(inputs["hy_w_in"][0]), cw=cw, cb=cb)
    maps = [dict(common, **_halo_inputs(x_full_T, zT, c)) for c in range(NCORES)]
    res = _launch(P, maps, "C")
    return np.ascontiguousarray(np.concatenate([r["uT"] for r in res], axis=1)), res[0]["uTc"]


def launch_C2(inputs, uT_full, uTc):
    P = Prog()
    P.pfx = "f_"
    P.internal = {"f_kf_l", "f_kb_l", "f_kf_c", "f_kb_c"}
    build_l2f(P=P)
    filt = {k: P.made["f_" + k] for k in ("kf_l", "kb_l", "kf_c", "kb_c")}
    P.pfx = "c_"
    build_l2c(P=P, filt=filt)
    P.finish()
    feT_l, tn_l = hyena_pos_tables(L)
    feT_c, tn_c = hyena_pos_tables(CTX)
    w3 = np.asarray(inputs["hy_filt_w3"][0], np.float32)
    vecs = np.stack([inputs["hy_filt_b1"][0], inputs["hy_filt_b2"][0], inputs["hy_filt_freq"][0]], axis=1).astype(np.float32)
    dl = hyena_deltas()
    w1p = np.concatenate([np.asarray(inputs["hy_filt_w1"][0], np.float32), np.zeros((HY_W - HY_EMB, HY_W), np.float32)], axis=0)
    fcommon = {"feT_l": feT_l, "tn_l": tn_l, "feT_c": feT_c, "tn_c": tn_c, "w1": np.ascontiguousarray(w1p),
               "w2": _f32(inputs["hy_filt_w2"][0]), "vecs": np.ascontiguousarray(vecs)}
    tabs = {"T_" + k: v for k, v in fft_tables().items()}
    skip = np.asarray(inputs["hy_skip"][0], np.float32)
    maps = []
    for c in range(NCORES):
        sl = slice(c * 128, (c + 1) * 128)
        up = np.zeros((128, L), np.float32)
        up[:, :CTX] = uTc[sl]
        U = np.concatenate([uT_full[sl], up], axis=0)
        m = _pf("f_", dict(fcommon, w3f=np.ascontiguousarray(w3[:, sl]), w3b=np.ascontiguousarray(w3[:, D + c * 128:D + (c + 1) * 128]),
                           ndelta=np.ascontiguousarray(-dl[sl].reshape(128, 1))))
        m.update(_pf("c_", dict(tabs, U=seq_to_abl(U), skipb=np.ascontiguousarray(np.concatenate([skip[sl], skip[sl]]).reshape(1, NSLOT)))))
        maps.append(m)
    res = _launch(P, maps, "C2")
    Y = [r["c_Y"].transpose(1, 0, 2).reshape(NSLOT, L) for r in res]
    yT_full = np.ascontiguousarray(np.concatenate([y[:128] for y in Y], axis=0))
    yTc = np.ascontiguousarray(np.concatenate([y[128:, :CTX] for y in Y], axis=0))
    return yT_full, yTc


def launch_D(inputs, modv, x_full_T, zT, yT_full, yTc, xT_shards):
    P = Prog()
    P.pfx = "d0_"
    P.internal = {"d0_uT", "d0_uTc", "d0_x0T", "d0_x0Tc"}
    build_l2a(P=P)
    P.bind["d1_aT"] = P.made["d0_x0T"]
    P.bind["d1_aTc"] = P.made["d0_x0Tc"]
    P.pfx = "d1_"
    build_gate_out("hy_w_out", 8, P=P)
    P.finish()
    cw, cb = _conv_lay(inputs["hy_conv_w"][0], inputs["hy_conv_b"][0])
    c0 = dict(_norms(inputs, modv, 2), hy_w_in=_f32(inputs["hy_w_in"][0]), cw=cw, cb=cb)
    c1 = dict(_norms(inputs, modv, 2), **_ffn(inputs, 2))
    c1.update({"bTc": yTc, "zT": zT, "hy_w_out": _f32(inputs["hy_w_out"][0])})
    ysh = shard_T(yT_full)
    maps = []
    for c in range(NCORES):
        m = _pf("d0_", dict(c0, **_halo_inputs(x_full_T, zT, c)))
        m.update(_pf("d1_", dict(c1, bT=ysh[c], xT=xT_shards[c])))
        maps.append(m)
    res = _launch(P, maps, "D")
    return [r["d1_xT_out"] for r in res], res[0]["d1_zT_out"]


L3A_OUTS = [n + t for t in ("", "c") for n in ("sz_tok", "xs_tok", "B_tok", "BT", "CT", "dt_tok")]
L3B_INS = ["xs_tok", "B_tok", "BT", "CT", "dt_tok", "xs_tokc", "B_tokc", "dt_tokc"]


def _l3a_inputs(inputs, modv, x_full_T, zT, c):
    cw, cb = _conv_lay(inputs["ssd_conv_w"][0], inputs["ssd_conv_b"][0])
    m = dict(_norms(inputs, modv, 3), ssd_w_in=_f32(inputs["ssd_w_in"][0]), cw=cw, cb=cb,
             dtb=_f32(np.asarray(inputs["ssd_dt_bias"][0]).reshape(1, 64)), ident=np.eye(128, dtype=np.float32))
    m.update(_halo_inputs(x_full_T, zT, c))
    return m


def _l3b_common(inputs):
    tf, tb = tri_consts()
    return {"alog": _f32(np.asarray(inputs["ssd_a_log"][0]).reshape(1, 64)), "tri_f": tf, "tri_b": tb}


def launch_E1(inputs, modv, x_full_T, zT):
    P = Prog()
    P.pfx = "e0_"
    P.internal = {"e0_" + n for n in L3A_OUTS}
    build_l3a(P=P)
    for n in L3B_INS:
        P.bind["e1_" + n] = P.made["e0_" + n]
    P.pfx = "e1_"
    build_l3b(False, P=P)
    P.finish()
    bc = _l3b_common(inputs)
    maps = []
    for c in range(NCORES):
        m = _pf("e0_", _l3a_inputs(inputs, modv, x_full_T, zT, c))
        m.update(_pf("e1_", bc))
        maps.append(m)
    res = _launch(P, maps, "E1")
    return [{"Hloc": r["e1_Hloc"], "Dtot": r["e1_Dtot"]} for r in res]


def launch_E2(inputs, modv, x_full_T, zT, summaries, xT_shards):
    P = Prog()
    P.pfx = "e0_"
    P.internal = {"e0_" + n for n in L3A_OUTS} | {"e1_yf_tok", "e1_yb_tok"}
    build_l3a(P=P)
    for n in L3B_INS:
        P.bind["e1_" + n] = P.made["e0_" + n]
    P.pfx = "e1_"
    build_l3b(True, P=P)
    P.bind["e2_yf_tok"] = P.made["e1_yf_tok"]
    P.bind["e2_yb_tok"] = P.made["e1_yb_tok"]
    P.bind["e2_xs_tok"] = P.made["e0_xs_tok"]
    P.bind["e2_sz_tok"] = P.made["e0_sz_tok"]
    P.pfx = "e2_"
    build_l3c(P=P)
    P.finish()
    bc = _l3b_common(inputs)
    bc["Hs"] = np.ascontiguousarray(np.stack([s["Hloc"] for s in summaries]))
    bc["Ds"] = np.ascontiguousarray(np.stack([s["Dtot"][:, 0, :] for s in summaries]).reshape(1, -1))
    c2 = dict(_norms(inputs, modv, 3), **_ffn(inputs, 3))
    c2.update({"dskip": np.ascontiguousarray(np.repeat(np.asarray(inputs["ssd_d_skip"][0], np.float32), 64).reshape(1, SI)),
               "normg": _f32(np.asarray(inputs["ssd_norm_g"][0]).reshape(1, SI)), "ident": np.eye(128, dtype=np.float32),
               "ssd_w_out": _f32(inputs["ssd_w_out"][0]), "final_g": lay_vec8(inputs["final_g"])})
    maps = []
    for c in range(NCORES):
        fm = np.zeros((1, 2 * NCORES), np.float32)
        fm[0, :c] = 1.0
        fm[0, NCORES + c + 1:] = 1.0
        m = _pf("e0_", _l3a_inputs(inputs, modv, x_full_T, zT, c))
        m.update(_pf("e1_", dict(bc, fm=fm)))
        m.update(_pf("e2_", dict(c2, xT=xT_shards[c])))
        maps.append(m)
    res = _launch(P, maps, "E2")
    return [r["e2_outT"] for r in res]


def kernel(**inputs):
    t00 = time.time()
    inputs = {k: np.asarray(v) for k, v in inputs.items()}
    t0 = time.time()
    modv = run_mod(inputs)
    _log("launch MOD: %.1f s" % (time.time() - t0))
    x = np.asarray(inputs["x"], np.float32)[0]
    xT_shards = [np.ascontiguousarray(x[i * TC:(i + 1) * TC].T) for i in range(NCORES)]
    zT = np.ascontiguousarray(np.asarray(inputs["ctx"], np.float32)[0].T)
    x0, z0, a_res = launch_A(inputs, modv, xT_shards, zT)
    t0 = time.time()
    x1, z1 = run_l1b(inputs, modv, x0, z0, a_res)
    _log("launch B: %.1f s" % (time.time() - t0))
    x1_full = np.ascontiguousarray(np.concatenate(x1, axis=1))
    uT_full, uTc = launch_C(inputs, modv, x1_full, z1)
    yT_full, yTc = launch_C2(inputs, uT_full, uTc)
    x2, z2 = launch_D(inputs, modv, x1_full, z1, yT_full, yTc, x1)
    x2_full = np.ascontiguousarray(np.concatenate(x2, axis=1))
    summ = launch_E1(inputs, modv, x2_full, z2)
    outs = launch_E2(inputs, modv, x2_full, z2, summ, x2)
    out = np.concatenate([o.T for o in outs], axis=0)[None]
    _log("kernel total %.1f s" % (time.time() - t00))
    return np.ascontiguousarray(out, dtype=np.float32)
```
